# Optimizing a Trainium2 kernel written in Bass

```python
import math
import jax, jax.numpy as jnp
from jax import lax
import numpy as np

D_MODEL = 1024
BATCH = 8
SEQ = 2048
DEPTH = 2
DEC_BATCH = 128
DEC_SEQ = 8
PAST_LEN = 16384
PAGE_SIZE = 128

DK_A = 128
DV_A = 128
H_A = D_MODEL // 128
KEY_A = H_A * DK_A
VAL_A = H_A * DV_A
CONV_W = 4
CONV_DIM = 2 * KEY_A + VAL_A
CHUNK = 64
HD_B = 64
H_B = D_MODEL // HD_B
H_KV = H_B // 4
G_B = H_B // H_KV
Q_B = H_B * HD_B
KV_B = H_KV * HD_B
WINDOW = 128
N_BUCKETS = 32
MAX_DISTANCE = 128
D_FF = 4 * D_MODEL
N_BRANCH = 2
N_IN = CONV_DIM + VAL_A + 2 * H_A + Q_B + 2 * KV_B + N_BRANCH * D_MODEL
ALPHA = (2 * DEPTH) ** 0.25
BETA_INIT = (8 * DEPTH) ** -0.25
LN_EPS = 1e-5
RMS_EPS = 1e-6

kernel_name = 'hybrid_gdn_swa_sink_decoder_step'


def layer_norm(x, g, b):
    xf = x.astype(jnp.float32)
    mu = jnp.mean(xf, -1, keepdims=True)
    var = jnp.mean(jnp.square(xf - mu), -1, keepdims=True)
    return ((xf - mu) * lax.rsqrt(var + LN_EPS) * g.astype(jnp.float32) + b.astype(jnp.float32)).astype(x.dtype)


def l2norm(x):
    return x * lax.rsqrt(jnp.sum(x * x, -1, keepdims=True) + 1e-6)


def t5_bucket(dist):
    n = jnp.maximum(dist, 0)
    max_exact = N_BUCKETS // 2
    large = max_exact + (jnp.log(jnp.maximum(n, max_exact).astype(jnp.float32) / max_exact)
                         / math.log(MAX_DISTANCE / max_exact) * (N_BUCKETS - max_exact)).astype(jnp.int32)
    large = jnp.minimum(large, N_BUCKETS - 1)
    return jnp.where(n < max_exact, n, large)


def short_conv(u, buf, w):
    L = u.shape[1]
    up = jnp.concatenate([buf.astype(u.dtype), u], axis=1)
    y = sum(up[:, j:j + L] * w[j] for j in range(CONV_W))
    return jax.nn.silu(y), up[:, -(CONV_W - 1):]


def gated_delta_rule(q, k, v, g, beta, s0):
    B, L, H, DK = q.shape
    DV = v.shape[-1]
    C = min(CHUNK, L)
    pad = (-L) % C
    if pad:
        padw = lambda t: jnp.pad(t, [(0, 0), (0, pad)] + [(0, 0)] * (t.ndim - 2))
        q, k, v, g, beta = (padw(t) for t in (q, k, v, g, beta))
    N = (L + pad) // C

    def blocks(t):
        return jnp.moveaxis(t.reshape(B, N, C, H, *t.shape[3:]), 3, 1)

    q, k, v, g, beta = (blocks(t) for t in (q, k, v, g, beta))
    g = jnp.cumsum(g, axis=-1)
    causal = jnp.tril(jnp.ones((C, C), bool))
    strict = causal & ~jnp.eye(C, dtype=bool)
    decay = jnp.exp(jnp.where(causal, g[..., :, None] - g[..., None, :], -jnp.inf))
    kb = k * beta[..., None]
    m = jnp.where(strict, jnp.einsum('bhncd,bhnsd->bhncs', kb, k) * decay, 0.0)
    a = m + jnp.eye(C, dtype=m.dtype)
    u = lax.linalg.triangular_solve(a, v * beta[..., None], left_side=True, lower=True, unit_diagonal=True)
    w = lax.linalg.triangular_solve(a, kb * jnp.exp(g)[..., None], left_side=True, lower=True, unit_diagonal=True)
    qk = jnp.einsum('bhncd,bhnsd->bhncs', q, k) * decay
    qg = q * jnp.exp(g)[..., None]
    kd = k * jnp.exp(g[..., -1:] - g)[..., None]
    gl = jnp.exp(g[..., -1])
    xs = tuple(jnp.moveaxis(t, 2, 0) for t in (u, w, qk, qg, kd, gl))

    def step(s, xs_i):
        u_i, w_i, qk_i, qg_i, kd_i, gl_i = xs_i
        v_new = u_i - jnp.einsum('bhcd,bhde->bhce', w_i, s)
        o_i = jnp.einsum('bhcd,bhde->bhce', qg_i, s) + jnp.einsum('bhcs,bhse->bhce', qk_i, v_new)
        s = s * gl_i[..., None, None] + jnp.einsum('bhcd,bhce->bhde', kd_i, v_new)
        return s, o_i

    s, o = lax.scan(step, s0, xs)
    o = jnp.transpose(o, (1, 0, 3, 2, 4)).reshape(B, N * C, H, DV)[:, :L]
    return o, s


def window_attention(q, k, v, q_pos, k_pos, sink, rel_bias):
    dist = q_pos[:, :, None] - k_pos[:, None, :]
    valid = (dist >= 0) & (dist < WINDOW) & (k_pos[:, None, :] >= 0)
    bias = rel_bias.astype(jnp.float32)[t5_bucket(dist)]
    bias = bias.reshape(*dist.shape, H_KV, G_B).transpose(0, 3, 4, 1, 2)
    s = jnp.einsum('bnqhgd,bnkhd->bnhgqk', q, k, preferred_element_type=jnp.float32) * (HD_B ** -0.5) + bias
    s = jnp.where(valid[:, None, None], s, -jnp.inf)
    sk = sink.astype(jnp.float32).reshape(H_KV, G_B, 1, 1)
    mx = jnp.maximum(jnp.max(s, -1, keepdims=True), sk)
    p = jnp.exp(s - mx)
    denom = jnp.sum(p, -1, keepdims=True) + jnp.exp(sk - mx)
    return jnp.einsum('bnhgqk,bnkhd->bnqhgd', (p / denom).astype(v.dtype), v)


def swa_prompt(q, k, v, sink, rel_bias):
    B, L = q.shape[:2]
    nb = L // WINDOW
    qb = q.reshape(B, nb, WINDOW, H_KV, G_B, HD_B)

    def band(t):
        tp = jnp.concatenate([jnp.zeros_like(t[:, :WINDOW]), t], 1).reshape(B, nb + 1, WINDOW, H_KV, HD_B)
        return jnp.concatenate([tp[:, :-1], tp[:, 1:]], axis=2)

    pos = jnp.arange(-WINDOW, L, dtype=jnp.int32).reshape(nb + 1, WINDOW)
    k_pos = jnp.concatenate([pos[:-1], pos[1:]], axis=1)
    o = window_attention(qb, band(k), band(v), pos[1:], k_pos, sink, rel_bias)
    return o.reshape(B, L, Q_B), k[:, -WINDOW:], v[:, -WINDOW:]


def swa_sample(q, k, v, k_buf, v_buf, sink, rel_bias):
    B, L = q.shape[:2]
    nbuf = k_buf.shape[1]
    kf = jnp.concatenate([k_buf.astype(k.dtype), k], axis=1)
    vf = jnp.concatenate([v_buf.astype(v.dtype), v], axis=1)
    q_pos = (PAST_LEN + jnp.arange(L, dtype=jnp.int32))[None]
    k_pos = (PAST_LEN - nbuf + jnp.arange(nbuf + L, dtype=jnp.int32))[None]
    o = window_attention(q.reshape(B, 1, L, H_KV, G_B, HD_B), kf[:, None], vf[:, None], q_pos, k_pos, sink, rel_bias)
    return o.reshape(B, L, Q_B), kf[:, -WINDOW:], vf[:, -WINDOW:]


def trunk_layer(x, c, p, rel_bias, conv_buf, s0, k_buf, v_buf):
    B, L, _ = x.shape
    f32 = jnp.float32
    mod = (jax.nn.silu(c) @ p['w_ada'] + p['b_ada'])[:, None, :]
    sh1, sc1, gt1, sh2, sc2, gt2 = jnp.split(mod, 6, axis=-1)
    h = x * (1 + sc1) + sh1
    proj = h @ p['w_in']
    offs = [int(o) for o in np.cumsum([CONV_DIM, VAL_A, H_A, H_A, Q_B, KV_B, KV_B])]
    conv_in, z, b_a, a_a, q_b, k_b, v_b, gate_logits = jnp.split(proj, offs, axis=-1)

    conv_out, new_conv = short_conv(conv_in, conv_buf, p['w_conv'])
    qa, ka, va = jnp.split(conv_out.astype(f32), [KEY_A, 2 * KEY_A], axis=-1)
    qa = l2norm(qa.reshape(B, L, H_A, DK_A)) * (DK_A ** -0.5)
    ka = l2norm(ka.reshape(B, L, H_A, DK_A))
    va = va.reshape(B, L, H_A, DV_A)
    beta = jax.nn.sigmoid(b_a.astype(f32))
    g = -jnp.exp(p['a_log'].astype(f32)) * jax.nn.softplus(a_a.astype(f32) + p['dt_bias'].astype(f32))
    o_a, s_new = gated_delta_rule(qa, ka, va, g, beta, s0.astype(f32))
    o_a = o_a * lax.rsqrt(jnp.mean(jnp.square(o_a), -1, keepdims=True) + RMS_EPS) * p['w_onorm'].astype(f32)
    o_a = (o_a * jax.nn.silu(z.astype(f32).reshape(B, L, H_A, DV_A))).reshape(B, L, VAL_A).astype(x.dtype)

    qh = q_b.reshape(B, L, H_B, HD_B)
    kh = k_b.reshape(B, L, H_KV, HD_B)
    vh = v_b.reshape(B, L, H_KV, HD_B)
    if k_buf is None:
        o_b, new_k, new_v = swa_prompt(qh, kh, vh, p['sinks'], rel_bias)
    else:
        o_b, new_k, new_v = swa_sample(qh, kh, vh, k_buf, v_buf, p['sinks'], rel_bias)

    gates = jax.nn.sigmoid(gate_logits.astype(f32)).astype(x.dtype)
    g_a, g_b = jnp.split(gates, N_BRANCH, axis=-1)
    mixed = g_a * (o_a @ p['w_pa']) + g_b * (o_b @ p['w_pb'])
    x = layer_norm(ALPHA * x + gt1 * (mixed @ p['w_out']), p['ln1_g'], p['ln1_b'])

    h2 = x * (1 + sc2) + sh2
    ff = jnp.square(jax.nn.relu(h2 @ p['w_up'])) @ p['w_down']
    x = layer_norm(ALPHA * x + gt2 * ff, p['ln2_g'], p['ln2_b'])
    return x, s_new.astype(s0.dtype), new_conv, new_k, new_v


def setup_inputs(seed: int = 0) -> dict:
    key = jax.random.key(seed)
    ks = iter(jax.random.split(key, 32))
    nrm = lambda shape, s: s * jax.random.normal(next(ks), shape, jnp.float32)
    d = D_MODEL
    col_scale = jnp.ones((N_IN,), jnp.float32)
    col_scale = col_scale.at[2 * KEY_A:CONV_DIM].set(BETA_INIT)
    vb0 = CONV_DIM + VAL_A + 2 * H_A + Q_B + KV_B
    col_scale = col_scale.at[vb0:vb0 + KV_B].set(BETA_INIT)
    dt = jnp.exp(jax.random.uniform(next(ks), (DEPTH, H_A), jnp.float32, math.log(1e-3), math.log(1e-1)))
    return {
        'x_prompt': nrm((BATCH, SEQ, d), 1.0),
        'x_sample': nrm((DEC_BATCH, DEC_SEQ, d), 1.0),
        'state_delta': nrm((DEPTH, DEC_BATCH, H_A, DK_A, DV_A), DK_A ** -0.5),
        'state_conv': nrm((DEPTH, DEC_BATCH, CONV_W - 1, CONV_DIM), 1.0),
        'cache_k': nrm((DEPTH, DEC_BATCH, WINDOW, H_KV, HD_B), 1.0),
        'cache_v': nrm((DEPTH, DEC_BATCH, WINDOW, H_KV, HD_B), 1.0),
        'c_prompt': nrm((BATCH, d), 1.0),
        'c_sample': nrm((DEC_BATCH, d), 1.0),
        'rel_bias': nrm((N_BUCKETS, H_B), 0.5),
        'w_ada': nrm((DEPTH, d, 6 * d), d ** -0.5),
        'b_ada': nrm((DEPTH, 6 * d), 0.01),
        'w_in': nrm((DEPTH, d, N_IN), d ** -0.5) * col_scale,
        'w_conv': nrm((DEPTH, CONV_W, CONV_DIM), 0.5),
        'a_log': jnp.log(jax.random.uniform(next(ks), (DEPTH, H_A), jnp.float32, 1.0, 16.0)),
        'dt_bias': jnp.log(jnp.expm1(dt)),
        'w_onorm': 1.0 + nrm((DEPTH, DV_A), 0.02),
        'sinks': nrm((DEPTH, H_B), 0.5),
        'w_pa': nrm((DEPTH, VAL_A, d), VAL_A ** -0.5 * BETA_INIT),
        'w_pb': nrm((DEPTH, Q_B, d), Q_B ** -0.5 * BETA_INIT),
        'w_out': nrm((DEPTH, d, d), d ** -0.5 * BETA_INIT),
        'ln1_g': 1.0 + nrm((DEPTH, d), 0.02),
        'ln1_b': nrm((DEPTH, d), 0.01),
        'w_up': nrm((DEPTH, d, D_FF), d ** -0.5 * BETA_INIT),
        'w_down': nrm((DEPTH, D_FF, d), D_FF ** -0.5 * BETA_INIT),
        'ln2_g': 1.0 + nrm((DEPTH, d), 0.02),
        'ln2_b': nrm((DEPTH, d), 0.01),
    }


def reference(x_prompt, x_sample, state_delta, state_conv, cache_k, cache_v, c_prompt, c_sample,
              rel_bias, w_ada, b_ada, w_in, w_conv, a_log, dt_bias, w_onorm, sinks,
              w_pa, w_pb, w_out, ln1_g, ln1_b, w_up, w_down, ln2_g, ln2_b):
    yp, ys = x_prompt, x_sample
    bp = x_prompt.shape[0]
    pd, pc, pk, pv, sd, sc, sk, sv = [], [], [], [], [], [], [], []
    for l in range(DEPTH):
        p = dict(w_ada=w_ada[l], b_ada=b_ada[l], w_in=w_in[l], w_conv=w_conv[l], a_log=a_log[l],
                 dt_bias=dt_bias[l], w_onorm=w_onorm[l], sinks=sinks[l], w_pa=w_pa[l], w_pb=w_pb[l],
                 w_out=w_out[l], ln1_g=ln1_g[l], ln1_b=ln1_b[l], w_up=w_up[l], w_down=w_down[l],
                 ln2_g=ln2_g[l], ln2_b=ln2_b[l])
        yp, s_p, conv_p, k_p, v_p = trunk_layer(
            yp, c_prompt, p, rel_bias,
            jnp.zeros((bp, CONV_W - 1, CONV_DIM), x_prompt.dtype),
            jnp.zeros((bp, H_A, DK_A, DV_A), state_delta.dtype), None, None)
        ys, s_s, conv_s, k_s, v_s = trunk_layer(
            ys, c_sample, p, rel_bias, state_conv[l], state_delta[l], cache_k[l], cache_v[l])
        pd.append(s_p); pc.append(conv_p); pk.append(k_p); pv.append(v_p)
        sd.append(s_s); sc.append(conv_s); sk.append(k_s); sv.append(v_s)
    return (yp, ys, jnp.stack(pd), jnp.stack(pc), jnp.stack(pk), jnp.stack(pv),
            jnp.stack(sd), jnp.stack(sc), jnp.stack(sk), jnp.stack(sv))
```

```python
import contextlib
import math
import numpy as np
import concourse.bass as bass
import concourse.mybir as mybir
from concourse.bass_utils import run_bass_kernel_spmd

F32 = mybir.dt.float32
BF16 = mybir.dt.bfloat16
AF = mybir.ActivationFunctionType
ALU = mybir.AluOpType
AX = mybir.AxisListType

NCORES = 8
D = 1024
T = 2176
NT = 17
TP = 2048
NIN = 7696
ALPHA = 4 ** 0.25
NEG = -30000.0
GROUPS = [(0, 512), (512, 512), (1024, 512), (1536, 512), (2048, 128)]


class Reg:
    __slots__ = ("w", "r", "excl")

    def __init__(self, excl=False):
        self.w = None
        self.r = {}
        self.excl = excl


class Sched:
    ENG = ("pe", "act", "dve", "pool", "sp")
    NSLOT = {"sp": 8, "pool": 8}

    def __init__(self, nc, st):
        self.nc = nc
        self.ops = {e: [] for e in self.ENG}
        self.cnt = {e: 0 for e in self.ENG}
        self.known = {e: {} for e in self.ENG}
        self.slot_next = {q: 0 for q in self.NSLOT}
        self.slot_cnt = {}
        self.sem = {}
        for e in self.ENG:
            self.sem[e] = st.enter_context(nc.semaphore("s_" + e))
        for q, n in self.NSLOT.items():
            for s in range(n):
                self.sem[("dma", q, s)] = st.enter_context(nc.semaphore("d_%s_%d" % (q, s)))
        self.nblocks = 0

    def _deps(self, eng, rd, wr):
        deps = {}

        def add(ev, same_ok):
            if ev is None:
                return
            k, c = ev
            if k == eng and not same_ok:
                return
            if c > deps.get(k, 0):
                deps[k] = c
        for r in rd:
            add(r.w, eng != "pe")
            if r.excl:
                for k, c in r.r.items():
                    add((k, c), False)
        for r in wr:
            add(r.w, eng != "pe")
            for k, c in r.r.items():
                add((k, c), eng != "pe")
        waits = []
        kn = self.known[eng]
        for k, c in deps.items():
            if c > kn.get(k, 0):
                kn[k] = c
                waits.append((k, c))
        return waits

    def _mark(self, ev, rd, wr):
        k, c = ev
        for r in rd:
            if c > r.r.get(k, 0):
                r.r[k] = c
        for r in wr:
            r.w = ev
            r.r = {}

    def op(self, eng, fn, rd=(), wr=()):
        waits = self._deps(eng, rd, wr)
        self.cnt[eng] += 1
        ev = (eng, self.cnt[eng])
        self.ops[eng].append((waits, fn, (eng, 1)))
        self._mark(ev, rd, wr)

    def dma(self, q, out, in_, rd=(), wr=()):
        waits = self._deps(q, rd, wr)
        s = self.slot_next[q]
        self.slot_next[q] = (s + 1) % self.NSLOT[q]
        key = ("dma", q, s)
        prev = self.slot_cnt.get(key, 0)
        kn = self.known[q]
        if prev > kn.get(key, 0):
            kn[key] = prev
            waits.append((key, prev))
        newc = prev + 16
        self.slot_cnt[key] = newc
        self.ops[q].append((waits, lambda e: e.dma_start(out=out, in_=in_), (key, 16)))
        self._mark((key, newc), rd, wr)

    def barrier(self):
        for e in self.ENG:
            waits = []
            kn = self.known[e]
            for key, c in self.slot_cnt.items():
                if c > kn.get(key, 0):
                    kn[key] = c
                    waits.append((key, c))
            for e2 in self.ENG:
                if e2 != e and self.cnt[e2] > kn.get(e2, 0):
                    kn[e2] = self.cnt[e2]
                    waits.append((e2, self.cnt[e2]))
            if waits:
                self.ops[e].append((waits, None, None))

    def flush(self):
        if not any(self.ops[e] for e in self.ENG):
            return
        self.barrier()
        nc = self.nc
        sem = self.sem
        ops = self.ops
        self.ops = {e: [] for e in self.ENG}
        self.nblocks += 1

        def run(e, lst):
            for waits, fn, inc in lst:
                for k, c in waits:
                    e.wait_ge(sem[k], c)
                if fn is not None:
                    fn(e).then_inc(sem[inc[0]], inc[1])
        with nc.Block() as block:
            @block.tensor
            def _(e):
                run(e, ops["pe"])

            @block.scalar
            def _(e):
                run(e, ops["act"])

            @block.vector
            def _(e):
                run(e, ops["dve"])

            @block.gpsimd
            def _(e):
                run(e, ops["pool"])

            @block.sync
            def _(e):
                run(e, ops["sp"])


def AP(t, off, dims):
    return bass.AP(t, off, [list(d) for d in dims])


class K:
    def __init__(self, stop=None, debug=False):
        self.stop = stop
        self.debug = debug
        self.nc = nc = bass.Bass("TRN2", target_bir_lowering=False)
        self.st = st = contextlib.ExitStack()
        self.S = Sched(nc, st)
        self.dr = {}
        self.ps = []
        self.psr = []
        self.ps_i = 0

    def din(self, name, shape, dt=F32):
        t = self.nc.dram_tensor(name, list(shape), dt, kind="ExternalInput")
        self.dr[name] = Reg()
        return t

    def dout(self, name, shape, dt=F32):
        t = self.nc.dram_tensor(name, list(shape), dt, kind="ExternalOutput")
        self.dr[name] = Reg()
        return t

    def dscr(self, name, shape, dt=F32):
        t = self.nc.dram_tensor(name, list(shape), dt, kind="ExternalOutput" if self.debug else "Internal")
        self.dr[name] = Reg()
        return t

    def sb(self, stack, name, shape, dt=F32):
        self.nsb = getattr(self, "nsb", 0) + 1
        t = stack.enter_context(self.nc.sbuf_tensor("%s_%d" % (name, self.nsb), list(shape), dt))
        return t, Reg()

    def free_probe(self, tag):
        for kb in range(200, 0, -2):
            try:
                with self.nc.sbuf_tensor("probe_%s_%d" % (tag, kb), [128, kb * 256], F32):
                    pass
                print("SBUF free at", tag, ":", kb, "KB")
                return
            except BaseException:
                continue
        print("SBUF free at", tag, ": <2KB")

    def bank(self, i=None):
        if i is None:
            i = self.rot[self.ps_i % len(self.rot)]
            self.ps_i += 1
        return self.ps[i], self.psr[i]

    def mm(self, out, lhsT, rhs, start, stop, rd, wr):
        self.S.op("pe", lambda e: e.matmul(out, lhsT=lhsT, rhs=rhs, start=start, stop=stop), rd=rd, wr=wr)

    def tr(self, out, in_, n, rd, wr):
        idn = self.ident[0:n, 0:n]
        self.S.op("pe", lambda e: e.transpose(out=out, in_=in_, identity=idn), rd=list(rd) + [self.identr], wr=wr)

    def act(self, out, in_, func, rd, wr, bias=0.0, scale=1.0, accum=None):
        if accum is None:
            self.S.op("act", lambda e: e.activation(out=out, in_=in_, func=func, bias=bias, scale=scale), rd=rd, wr=wr)
        else:
            self.S.op("act", lambda e: e.activation(out=out, in_=in_, func=func, bias=bias, scale=scale,
                                                    accum_out=accum), rd=rd, wr=wr)

    def tt(self, eng, out, in0, in1, op, rd, wr):
        self.S.op(eng, lambda e: e.tensor_tensor(out=out, in0=in0, in1=in1, op=op), rd=rd, wr=wr)

    def ts(self, eng, out, in0, s1, s2, op0, op1, rd, wr):
        if s2 is None:
            self.S.op(eng, lambda e: e.tensor_scalar(out=out, in0=in0, scalar1=s1, scalar2=None, op0=op0), rd=rd, wr=wr)
        else:
            self.S.op(eng, lambda e: e.tensor_scalar(out=out, in0=in0, scalar1=s1, scalar2=s2, op0=op0, op1=op1),
                      rd=rd, wr=wr)

    def stt(self, eng, out, in0, sc, in1, op0, op1, rd, wr):
        self.S.op(eng, lambda e: e.scalar_tensor_tensor(out=out, in0=in0, scalar=sc, in1=in1, op0=op0, op1=op1),
                  rd=rd, wr=wr)

    def cp(self, eng, out, in_, rd, wr):
        if eng == "act":
            self.S.op("act", lambda e: e.copy(out=out, in_=in_), rd=rd, wr=wr)
        else:
            self.S.op(eng, lambda e: e.tensor_copy(out=out, in_=in_), rd=rd, wr=wr)

    def ms(self, eng, out, val, wr):
        self.S.op(eng, lambda e: e.memset(out, val), wr=wr)

    def dma(self, out, in_, rd, wr, q="sp"):
        self.S.dma(q, out, in_, rd=rd, wr=wr)

    def rows_T(self, stack, src_ap, srcreg, R, C, dst, dstreg, name):
        rowt, rowr = self.sb(stack, name, [R, C])
        self.dma(rowt[:], src_ap, [srcreg], [rowr])
        nch = C // 128
        per = 512 // R
        c = 0
        while c < nch:
            n = min(per, nch - c)
            pb, pr = self.bank()
            for i in range(n):
                self.tr(pb[:, i * R:(i + 1) * R], rowt[:, (c + i) * 128:(c + i + 1) * 128], R, [rowr], [pr])
            self.cp("dve", dst[:, c:c + n, :], pb[:, 0:n * R].rearrange("p (a b) -> p a b", a=n), [pr], [dstreg])
            c += n

    def build(self):
        nc, S, st = self.nc, self.S, self.st
        x_tok = self.din("x_tok", [T, D])
        c_all = self.din("c_all", [17, D])
        sd_in = self.din("sd_in", [2, 16, 8, 128, 128])
        sc_in = self.din("sc_in", [2, 48, 3072])
        ck_in = self.din("ck_in", [2, 16, 128, 256])
        cv_in = self.din("cv_in", [2, 16, 128, 256])
        rel_bias = self.din("rel_bias", [32, 16])
        w_ada = self.din("w_ada", [2, D, 6144])
        b_ada = self.din("b_ada", [2, 6144])
        w_in = self.din("w_in", [2, D, NIN])
        w_conv = self.din("w_conv", [8, 3072])
        a_log = self.din("a_log", [1, 16])
        dt_bias = self.din("dt_bias", [1, 16])
        w_onorm = self.din("w_onorm", [1, 256])
        sinks = self.din("sinks", [1, 32])
        w_pa = self.din("w_pa", [2, D, D])
        w_pb = self.din("w_pb", [2, D, D])
        w_out = self.din("w_out", [2, D, D])
        ln_all = self.din("ln_all", [8, D])
        w_up = self.din("w_up", [2, D, 4096])
        w_down = self.din("w_down", [2, 4096, D])
        cmat = self.din("cmat", [16, 128, 128])
        csmall = self.din("csmall", [128, 16 + 48])
        csel = self.din("csel", [8, 8 * 128])
        coh = self.din("coh", [32, 384])
        cneg = self.din("cneg", [16, 384])

        y_tok = self.dout("y_tok", [T, D])
        p_sd = self.dout("p_sd", [2, 8, 128, 128])
        p_sc = self.dout("p_sc", [2, 3, 3072])
        p_ck = self.dout("p_ck", [2, 128, 256])
        p_cv = self.dout("p_cv", [2, 128, 256])
        s_sd = self.dout("s_sd", [2, 16, 8, 128, 128])
        s_sc = self.dout("s_sc", [2, 48, 3072])
        s_ck = self.dout("s_ck", [2, 16, 128, 256])
        s_cv = self.dout("s_cv", [2, 16, 128, 256])

        xs = self.dscr("xs", [128, 8, T])
        oa = self.dscr("oa", [8, 128, T], BF16)
        ob = self.dscr("ob", [8, 128, T], BF16)
        Z = self.dscr("Z", [16, 128, 384])
        ZS = self.dscr("ZS", [128, 16, 256])
        wsc = self.dscr("wsc", [2, 26, 128, 8, 512], BF16)
        self.wscr = [[Reg() for _ in range(26)] for _ in range(2)]
        dr = self.dr

        for i in range(8):
            t = st.enter_context(nc.psum_tensor("pb%d" % i, [128, 512], F32))
            self.ps.append(t)
            self.psr.append(Reg(excl=True))
        self.rot = list(range(8))

        P = contextlib.ExitStack()
        st.enter_context(P)
        cm, cmr = self.sb(P, "cm", [128, 16, 128])
        self.ident = cm[:, 0, :]
        self.identr = cmr
        ident = cm[:, 0, :]
        ones = cm[:, 1, :]
        Umat = {0: cm[:, 2, :], 1: cm[:, 4, :]}
        Bmat = {0: cm[:, 3, :], 1: cm[:, 5, :]}
        NEGLm = {0: cm[:, 6, :], 1: cm[:, 7, :]}
        onesm = cm[:, 8, :]
        bd16 = cm[:, 9, :]
        mo = {16: cm[:, 10, :], 32: cm[:, 11, :], 64: cm[:, 12, :]}
        moT = {16: cm[:, 13, :], 32: cm[:, 14, :], 64: cm[:, 15, :]}
        NEGUm = {}
        negu, negur = self.sb(P, "negu", [128, 2, 128])
        NEGUm[0] = negu[:, 0, :]
        NEGUm[1] = negu[:, 1, :]
        csm, csmr = self.sb(P, "csm", [128, 64])
        blkm = csm[:, 0:16]
        selg = csm[:, 16:64]
        sel, selr = self.sb(P, "sel", [8, 8 * 128])
        hT, hTr = self.sb(P, "hT", [128, 8, T], BF16)
        hTg = [Reg() for _ in range(5)]
        modT, modr = self.sb(P, "modT", [128, 2, 48, 17])
        lnp, lnpr = self.sb(P, "lnp", [128, 8, 8])
        wcv, wcvr = self.sb(P, "wcv", [128, 24, 8])
        dtb, dtbr = self.sb(P, "dtb", [128, 16])
        nA, nAr = self.sb(P, "nA", [128, 16])
        wonb, wonbr = self.sb(P, "wonb", [128, 256])
        snk, snkr = self.sb(P, "snk", [128, 32])

        def grp_of(t0):
            return min(t0 // 512, 4)

        with contextlib.ExitStack() as ph:
            self.dma(cm[:], cmat.ap().rearrange("c p n -> p c n"), [dr["cmat"]], [cmr])
            self.dma(csm[:], csmall.ap(), [dr["csmall"]], [csmr])
            self.dma(sel[:], csel.ap(), [dr["csel"]], [selr])
            self.dma(dtb[:], AP(dt_bias, 0, [[0, 128], [1, 16]]), [dr["dt_bias"]], [dtbr])
            self.dma(nA[:], AP(a_log, 0, [[0, 128], [1, 16]]), [dr["a_log"]], [nAr])
            self.dma(wonb[:], AP(w_onorm, 0, [[0, 128], [1, 256]]), [dr["w_onorm"]], [wonbr])
            self.dma(snk[:], AP(sinks, 0, [[0, 128], [1, 32]]), [dr["sinks"]], [snkr])
            self.act(nA[:], nA[:], AF.Exp, [nAr], [nAr])
            self.ts("dve", nA[:], nA[:], -1.0, None, ALU.mult, None, [nAr], [nAr])
            for v in range(2):
                pb, pr = self.bank()
                self.tr(pb[:, 0:128], NEGLm[v], 128, [cmr], [pr])
                self.cp("dve", NEGUm[v], pb[:, 0:128], [pr], [negur])
            if self.stop == "s1":
                S.flush()
                return nc
            rb, rbr = self.sb(ph, "rb", [32, 16])
            oh, ohr = self.sb(ph, "oh", [32, 384])
            ngm, ngmr = self.sb(ph, "ngm", [16, 384])
            fv, fvr = self.sb(ph, "fv", [16, 384])
            self.dma(rb[:], rel_bias.ap(), [dr["rel_bias"]], [rbr])
            self.dma(oh[:], coh.ap(), [dr["coh"]], [ohr])
            self.dma(ngm[:], cneg.ap(), [dr["cneg"]], [ngmr])
            pb, pr = self.bank()
            self.mm(pb[0:16, 0:384], rb[:], oh[:], True, True, [rbr, ohr], [pr])
            self.tt("dve", fv[:], pb[0:16, 0:384], ngm[:], ALU.add, [pr, ngmr], [fvr])
            self.dma(Z.ap(), AP(fv, 0, [[384, 16], [0, 128], [1, 384]]), [fvr], [dr["Z"]])
            bs, bsr = self.sb(ph, "bs", [128, 16, 256])
            self.ms("pool", bs[:], NEG, [bsr])
            for b in range(16):
                self.dma(bs[8 * b:8 * b + 8, :, 0:128], AP(Z, 127, [[383, 8], [128 * 384, 16], [1, 128]]),
                         [dr["Z"]], [bsr])
                self.dma(bs[8 * b:8 * b + 8, :, 128 + 8 * b:136 + 8 * b],
                         AP(Z, 255, [[383, 8], [128 * 384, 16], [1, 8]]), [dr["Z"]], [bsr])
            self.dma(ZS.ap(), bs[:], [bsr], [dr["ZS"]])
            if self.stop == "s2":
                S.flush()
                return nc
            self.rows_T(ph, ln_all.ap(), dr["ln_all"], 8, D, lnp, lnpr, "r_ln")
            self.rows_T(ph, w_conv.ap(), dr["w_conv"], 8, 3072, wcv, wcvr, "r_wc")
            bad, badr = self.sb(ph, "bad", [128, 48, 2])
            self.rows_T(ph, b_ada.ap(), dr["b_ada"], 2, 6144, bad, badr, "r_ba")
            if self.stop == "s3":
                S.flush()
                return nc
            cT32, cT32r = self.sb(ph, "cT32", [128, 8, 17])
            crow, crowr = self.sb(ph, "crow", [17, D])
            self.dma(crow[:], c_all.ap(), [dr["c_all"]], [crowr])
            self.act(crow[:], crow[:], AF.Silu, [crowr], [crowr])
            pb, pr = self.bank()
            for k in range(8):
                self.tr(pb[:, k * 17:(k + 1) * 17], crow[:, k * 128:(k + 1) * 128], 17, [crowr], [pr])
            csT, csTr = self.sb(ph, "csT", [128, 8, 17], BF16)
            self.cp("dve", csT[:], pb[:, 0:136].rearrange("p (a b) -> p a b", a=8), [pr], [csTr])
            wa = [self.sb(ph, "wa%d" % i, [128, 8, 512], BF16) for i in range(2)]
            n = 0
            for l in range(2):
                for g in range(12):
                    wt, wr_ = wa[n % 2]
                    n += 1
                    self.dma(wt[:], w_ada.ap()[l].rearrange("(k p) n -> p k n", p=128)[:, :, g * 512:(g + 1) * 512],
                             [dr["w_ada"]], [wr_], q="pool")
                    pb, pr = self.bank()
                    for mmi in range(4):
                        for k in range(8):
                            self.mm(pb[:, mmi * 17:(mmi + 1) * 17], wt[:, k, mmi * 128:(mmi + 1) * 128], csT[:, k, :],
                                    k == 0, k == 7, [wr_, csTr], [pr])
                    self.tt("dve", modT[:, l, g * 4:(g + 1) * 4, :],
                            pb[:, 0:68].rearrange("p (a b) -> p a b", a=4),
                            AP(bad, (g * 4) * 2 + l, [[96, 128], [2, 4], [0, 17]]), ALU.add, [pr, badr], [modr])
                for c0 in (8, 32):
                    self.ts("dve", modT[:, l, c0:c0 + 8, :], modT[:, l, c0:c0 + 8, :], 1.0, None, ALU.add, None,
                            [modr], [modr])
            if self.stop == "s4":
                S.flush()
                return nc
            xin = [self.sb(ph, "xin%d" % i, [128, D]) for i in range(2)]
            xst = [self.sb(ph, "xst%d" % i, [128, 8, 128]) for i in range(2)]
            for t in range(NT):
                xi, xir = xin[t % 2]
                xo, xor_ = xst[t % 2]
                self.dma(xi[:], x_tok.ap()[t * 128:(t + 1) * 128, :], [dr["x_tok"]], [xir])
                for half in range(2):
                    pb, pr = self.bank()
                    for kk in range(4):
                        k = half * 4 + kk
                        self.tr(pb[:, kk * 128:(kk + 1) * 128], xi[:, k * 128:(k + 1) * 128], 128, [xir], [pr])
                    self.cp("act", xo[:, half * 4:half * 4 + 4, :], pb[:].rearrange("p (a b) -> p a b", a=4), [pr], [xor_])
                if self.stop != "s6":
                    self.dma(xs.ap()[:, :, t * 128:(t + 1) * 128], xo[:], [xor_], [dr["xs"]])
                if self.stop not in ("s5", "s6") and not (self.stop == "s7" and t == 16):
                    self.modulate(0, 0, xo, xor_, t * 128, 128, hT, hTg[grp_of(t * 128)], modT, modr, ph if t == 0 else None)
            S.flush()
            if self.stop in ("s5", "s6", "s7", "s8", "s9"):
                return nc

        self.__dict__.update({k_: v_ for k_, v_ in locals().items() if k_ not in ("self", "ph", "P")})
        stages = []
        for l in range(2):
            stages += [("gdn%d" % l, self.gdn_phase, l), ("swa%d" % l, self.swa_phase, l), ("mlp%d" % l, self.mlp_phase, l)]
        if self.stop != "setup":
            for nm, fn, l in stages:
                fn(l)
                if self.stop == nm or (self.stop or "").startswith("w") and nm == "swa0":
                    break
        S.flush()
        self.st.close()
        return nc

    def mlp_pieces(self, l):
        w_in_l = self.w_in.ap()[l].rearrange("(k p) n -> p k n", p=128)
        wsrc = lambda w: w.ap()[l].rearrange("(k p) n -> p k n", p=128)
        GA0, GB0 = 5648, 6672
        out = []
        for mh in range(2):
            cs = slice(mh * 512, (mh + 1) * 512)
            out.append((("pa", mh), wsrc(self.w_pa)[:, :, cs], "w_pa"))
            out.append((("ga", mh), w_in_l[:, :, GA0 + mh * 512:GA0 + (mh + 1) * 512], "w_in"))
            out.append((("pb", mh), wsrc(self.w_pb)[:, :, cs], "w_pb"))
            out.append((("gb", mh), w_in_l[:, :, GB0 + mh * 512:GB0 + (mh + 1) * 512], "w_in"))
        for mh in range(2):
            out.append((("wo", mh), wsrc(self.w_out)[:, :, mh * 512:(mh + 1) * 512], "w_out"))
        for fg in range(8):
            out.append((("wu", fg), wsrc(self.w_up)[:, :, fg * 512:(fg + 1) * 512], "w_up"))
        wdv = self.w_down.ap()[l].rearrange("(fc p) n -> p fc n", p=128)
        for mh in range(2):
            for fg in range(4):
                out.append((("wd", mh, fg), wdv[:, fg * 8:(fg + 1) * 8, mh * 512:(mh + 1) * 512], "w_down"))
        return out

    def modulate(self, l, which, src, srcr, t0, tw, dst, dstr, modT, modr, alloc_stack):
        sh0 = 0 if which == 0 else 24
        sc0 = 8 if which == 0 else 32
        if alloc_stack is not None:
            self.modtmp, self.modtmpr = self.sb(alloc_stack, "modtmp", [128, 128])
        if t0 < TP:
            for k in range(8):
                self.ts("dve", dst[:, k, t0:t0 + tw], src[:, k, 0:tw], modT[:, l, sc0 + k, 0:1], modT[:, l, sh0 + k, 0:1],
                        ALU.mult, ALU.add, [srcr, modr], [dstr])
        else:
            tmp, tmpr = self.modtmp, self.modtmpr
            for k in range(8):
                scb = AP(modT, l * 48 * 17 + (sc0 + k) * 17 + 1, [[2 * 48 * 17, 128], [1, 16], [0, 8]])
                shb = AP(modT, l * 48 * 17 + (sh0 + k) * 17 + 1, [[2 * 48 * 17, 128], [1, 16], [0, 8]])
                self.tt("dve", tmp[:].rearrange("p (b i) -> p b i", b=16), src[:, k, 0:128].rearrange("p (b i) -> p b i", b=16),
                        scb, ALU.mult, [srcr, modr], [tmpr])
                if self.stop == "s8":
                    continue
                self.tt("dve", dst[:, k, t0:t0 + 128].rearrange("p (b i) -> p b i", b=16),
                        tmp[:].rearrange("p (b i) -> p b i", b=16), shb, ALU.add, [tmpr, modr], [dstr])

    def gdn_phase(self, l):
        S, dr = self.S, self.dr
        hT, hTg, cm = self.hT, self.hTg, self.cm
        cmr = self.cmr
        ident, ones = self.ident, self.ones
        w_in_l = self.w_in.ap()[l].rearrange("(k p) n -> p k n", p=128)
        hrd = list(hTg)
        with contextlib.ExitStack() as ph:
            sb = lambda name, shape, dt=F32: self.sb(ph, name, shape, dt)
            wba, wbar = sb("wba", [128, 8, 16], BF16)
            self.dma(wba[:], w_in_l[:, :, 4096:4112], [dr["w_in"]], [wbar], q="pool")
            beta, betar = sb("beta", [128, NT, 8])
            gtok, gtokr = sb("gtok", [128, NT, 8])
            eg, egr = sb("eg", [128, NT, 8])
            egrev, egrevr = sb("egrev", [128, NT, 8])
            negegb, negegbr = sb("negegb", [128, NT, 8])
            glb, glbr = sb("glb", [128, 128])
            glS, glSr = sb("glS", [128, 128])
            tmpa, tmpar = sb("tmpa", [128, NT, 8])
            pb, pr = self.bank()
            for t in range(NT):
                for k in range(8):
                    self.mm(pb[:, t * 16:(t + 1) * 16], hT[:, k, t * 128:(t + 1) * 128], wba[:, k, :], k == 0, k == 7,
                            hrd + [wbar], [pr])
            pv = pb[:, 0:NT * 16].rearrange("p (t c) -> p t c", t=NT)
            self.act(beta[:], pv[:, :, 0:8], AF.Sigmoid, [pr], [betar])
            self.tt("dve", tmpa[:], pv[:, :, 8:16], AP(self.dtb, l * 8, [[16, 128], [0, NT], [1, 8]]), ALU.add,
                    [pr, self.dtbr], [tmpar])
            self.act(tmpa[:], tmpa[:], AF.Exp, [tmpar], [tmpar])
            self.act(tmpa[:], tmpa[:], AF.Ln, [tmpar], [tmpar], bias=1.0)
            self.tt("dve", gtok[:], tmpa[:], AP(self.nA, l * 8, [[16, 128], [0, NT], [1, 8]]), ALU.mult,
                    [tmpar, self.nAr], [gtokr])
            pb1, pr1 = self.bank()
            pb2, pr2 = self.bank()
            for t in range(NT):
                v = 1 if t == 16 else 0
                self.mm(pb1[:, t * 8:(t + 1) * 8], self.Umat[v], gtok[:, t, :], True, True, [cmr, gtokr], [pr1])
                self.mm(pb2[:, t * 8:(t + 1) * 8], self.Bmat[v], gtok[:, t, :], True, True, [cmr, gtokr], [pr2])
            self.act(eg[:], pb1[:, 0:NT * 8].rearrange("p (t c) -> p t c", t=NT), AF.Exp, [pr1], [egr])
            self.act(egrev[:], pb2[:, 0:NT * 8].rearrange("p (t c) -> p t c", t=NT), AF.Exp, [pr2], [egrevr])
            self.stt("dve", negegb[:], eg[:], -1.0, beta[:], ALU.mult, ALU.mult, [egr, betar], [negegbr])
            negbeta, negbetar = sb("negbeta", [128, NT, 8])
            self.ts("dve", negbeta[:], beta[:], -1.0, None, ALU.mult, None, [betar], [negbetar])
            pb, pr = self.bank()
            self.mm(pb[:, 0:128], ones, gtok[:, 0:16, :], True, True, [cmr, gtokr], [pr])
            self.act(glb[:], pb[:, 0:128], AF.Exp, [pr], [glbr])
            gm, gmr = sb("gm", [128, 8, 16])
            self.tt("dve", gm[:], AP(gtok, 16 * 8, [[NT * 8, 128], [1, 8], [0, 16]]),
                    AP(self.csm, 0, [[64, 128], [0, 8], [1, 16]]), ALU.mult, [gtokr, self.csmr], [gmr])
            pb, pr = self.bank()
            self.mm(pb[:, 0:128], ones, gm[:].rearrange("p a b -> p (a b)"), True, True, [cmr, gmr], [pr])
            self.act(glS[:], pb[:, 0:128], AF.Exp, [pr], [glSr])
            scr, scrr = sb("scr", [48, 3, 128])

            wg, wgr = sb("wg", [128, 8, 4, 128], BF16)
            UW = 3 + TP + 16 * 11
            ub = [sb("u%d" % i, [128, UW]) for i in range(2)]
            cb = [sb("c%d" % i, [128, T]) for i in range(3)]
            oaT, oaTr = sb("oaT", [128, T], BF16)
            Sst, Sstr = sb("Sst", [128, 128])
            Sb, Sbr = sb("Sb", [128, 16, 128])
            kTm, kTmr = sb("kTm", [128, 16, 128])
            qTm, qTmr = sb("qTm", [128, 16, 128])
            kdm, kdmr = kTm, kTmr
            hs, hsr = sb("hs", [128, 8, NT])
            cst, cstr = sb("cst", [128, 2, 3, 128])
            cso, csor = sb("cso", [48, 384])
            NS = 4
            slots = []
            for i in range(NS):
                d_ = {}
                for nm in ["A1", "Ds", "DTs", "DTi", "X0", "Y0", "P0", "X1", "Y1", "P1", "XF", "YF", "T0", "T1", "W", "bv", "kegb"]:
                    d_[nm] = sb("m%d_%s" % (i, nm), [128, 128])
                d_["Xo"], d_["Yo"], d_["Gs"], d_["Fs"] = d_["A1"], d_["Ds"], d_["DTs"], d_["DTi"]
                slots.append(d_)
            outs = []
            for par in range(2):
                row = []
                for i in range(NS):
                    row.append({nm: sb("o%d%d_%s" % (par, i, nm), [128, 128]) for nm in ["U0", "WmT", "QK", "kd", "zw"]})
                outs.append(row)
            mats = {nm: sb("m_" + nm, [128, 128]) for nm in ["Xp", "vn", "t1", "o", "of", "junk"]}
            sm, smr = sb("sm", [128, 4])
            if self.debug:
                self.free_probe("gdn")
            for i in range(2):
                self.ms("pool", ub[i][0][:, 0:3], 0.0, [ub[i][1]])
            self.ms("pool", qTm[:], 0.0, [qTmr])

            for h in range(8):
                cols = [h * 128, 1024 + h * 128, 2048 + h * 128, 3072 + h * 128]
                for s in range(4):
                    self.dma(wg[:, :, s, :], w_in_l[:, :, cols[s]:cols[s] + 128], [dr["w_in"]], [wgr], q="pool")
                self.dma(Sb[:], self.sd_in.ap()[l, :, h].rearrange("b k v -> k b v"), [dr["sd_in"]], [Sbr])
                pcs = self.mlp_pieces(l)
                for pi in range(h * 4, min(h * 4 + 4, 26)):
                    self.dma(self.wsc.ap()[l, pi], pcs[pi][1], [dr[pcs[pi][2]]], [self.wscr[l][pi]], q="pool")
                self.dma(scr[:], AP(self.sc_in, l * 48 * 3072 + h * 128, [[3072, 48], [1024, 3], [1, 128]]),
                         [dr["sc_in"]], [scrr])
                self.ms("pool", Sst[:], 0.0, [Sstr])
                for s in range(3):
                    u, ur = ub[s % 2]
                    self.ms("pool", u[:, 0:3], 0.0, [ur])
                    for (g0, gw) in GROUPS:
                        pb, pr = self.bank()
                        for k in range(8):
                            self.mm(pb[:, 0:gw], wg[:, k, s, :], hT[:, k, g0:g0 + gw], k == 0, k == 7, hrd + [wgr], [pr])
                        if g0 < TP:
                            self.cp("act", u[:, 3 + g0:3 + g0 + gw], pb[:, 0:gw], [pr], [ur])
                        else:
                            self.cp("act", AP(u, 3 + TP + 3, [[UW, 128], [11, 16], [1, 8]]),
                                    pb[:, 0:128].rearrange("p (b i) -> p b i", b=16), [pr], [ur])
                    pb, pr = self.bank()
                    ch = s * 8 + h
                    self.tr(pb[:, 0:48], scr[:, s, :], 48, [scrr], [pr])
                    self.cp("dve", AP(u, 3 + TP, [[UW, 128], [11, 16], [1, 3]]),
                            pb[:, 0:48].rearrange("p (b j) -> p b j", b=16), [pr], [ur])
                    c, cr = cb[s]
                    for j in range(4):
                        wj = self.wcv[:, ch, l * 4 + j:l * 4 + j + 1]
                        srcp = u[:, j:j + TP]
                        srcs = AP(u, 3 + TP + j, [[UW, 128], [11, 16], [1, 8]])
                        dsts = c[:, TP:T].rearrange("p (b i) -> p b i", b=16)
                        if j == 0:
                            self.ts("dve", c[:, 0:TP], srcp, wj, None, ALU.mult, None, [ur, self.wcvr], [cr])
                            self.ts("dve", dsts, srcs, wj, None, ALU.mult, None, [ur, self.wcvr], [cr])
                        else:
                            self.stt("dve", c[:, 0:TP], srcp, wj, c[:, 0:TP], ALU.mult, ALU.add, [ur, self.wcvr, cr], [cr])
                            self.stt("dve", dsts, srcs, wj, dsts, ALU.mult, ALU.add, [ur, self.wcvr, cr], [cr])
                    self.act(c[:], c[:], AF.Silu, [cr], [cr])
                for ti, t in enumerate((15, 16)):
                    pb, pr = self.bank()
                    for s in range(3):
                        for k in range(8):
                            self.mm(pb[:, s * 128:(s + 1) * 128], hT[:, k, t * 128:(t + 1) * 128], wg[:, k, s, :],
                                    k == 0, k == 7, hrd + [wgr], [pr])
                    self.cp("act", cst[:, ti, :, :], pb[:, 0:384].rearrange("p (s c) -> p s c", s=3), [pr], [cstr])
                self.dma(AP(self.p_sc, l * 3 * 3072 + h * 128, [[3072, 3], [1024, 3], [1, 128]]), cst[125:128, 0, :, :],
                         [cstr], [dr["p_sc"]])
                pb, pr = self.bank()
                self.mm(pb[0:48, 0:384], self.selg, cst[:, 1, :, :].rearrange("p s c -> p (s c)"), True, True,
                        [self.csmr, cstr], [pr])
                self.cp("dve", cso[:], pb[0:48, 0:384], [pr], [csor])
                self.dma(AP(self.s_sc, l * 48 * 3072 + h * 128, [[3072, 48], [1024, 3], [1, 128]]),
                         cso[:].rearrange("p (s c) -> p s c", s=3), [csor], [dr["s_sc"]])
                (cq, cqr), (ck, ckr), (cv, cvr) = cb
                pbn, prn = self.bank()
                for qi, (c, cr) in enumerate(((cq, cqr), (ck, ckr))):
                    u, ur = ub[qi]
                    self.act(u[:, 0:T], c[:], AF.Square, [cr], [ur])
                    for t in range(NT):
                        self.mm(pbn[:, qi * NT + t:qi * NT + t + 1], u[:, t * 128:(t + 1) * 128], ones[:, 0:1], True, True,
                                [ur, cmr], [prn])
                self.act(hs[:, 0:2, :], pbn[:, 0:2 * NT].rearrange("p (a t) -> p a t", a=2), AF.Sqrt, [prn], [hsr], bias=1e-6)
                self.S.op("dve", lambda e: e.reciprocal(out=hs[:, 0:2, :], in_=hs[:, 0:2, :]), rd=[hsr], wr=[hsr])
                self.tt("dve", hs[:, 2, :], hs[:, 1, :], hs[:, 1, :], ALU.mult, [hsr], [hsr])
                self.ts("dve", hs[:, 0, :], hs[:, 0, :], 128 ** -0.5, None, ALU.mult, None, [hsr], [hsr])
                col = lambda tns: AP(tns, h, [[NT * 8, 128], [8, NT]])
                self.tt("dve", hs[:, 3, :], col(negbeta), hs[:, 2, :], ALU.mult, [negbetar, hsr], [hsr])
                self.tt("dve", hs[:, 4, :], col(beta), hs[:, 1, :], ALU.mult, [betar, hsr], [hsr])
                self.tt("dve", hs[:, 5, :], col(negegb), hs[:, 2, :], ALU.mult, [negegbr, hsr], [hsr])
                self.tt("dve", hs[:, 6, :], col(eg), hs[:, 0, :], ALU.mult, [egr, hsr], [hsr])
                self.ms("pool", kTm[:], 0.0, [kTmr])
                Lc = locals()
                groups = [list(range(g0_, min(g0_ + NS, NT))) for g0_ in range(0, NT, NS)]

                def lockstep(gens):
                    gens = list(gens)
                    while gens:
                        for g_ in list(gens):
                            try:
                                next(g_)
                            except StopIteration:
                                gens.remove(g_)
                prev = None
                for gi, grp in enumerate(groups):
                    gens = [self.gdn_solve(l, h, t, slots[i], outs[gi % 2][i], Lc) for i, t in enumerate(grp)]
                    if prev is not None:
                        gens.append(self.gdn_state(l, h, prev[0], outs[prev[1] % 2], Lc))
                    lockstep(gens)
                    prev = (grp, gi)
                lockstep([self.gdn_state(l, h, prev[0], outs[prev[1] % 2], Lc)])
                self.dma(self.p_sd.ap()[l, h], Sst[:], [Sstr], [dr["p_sd"]])
                self.dma(self.s_sd.ap()[l, :, h].rearrange("b k v -> k b v"), Sb[:], [Sbr], [dr["s_sd"]])
                self.dma(self.oa.ap()[h], oaT[:], [oaTr], [dr["oa"]])
            S.flush()

    def gdn_solve(self, l, h, t, m, O, L):
        cmr = self.cmr
        ident = self.ident
        (cq, cqr), (ck, ckr), (cv, cvr) = L["cb"]
        tc = slice(t * 128, (t + 1) * 128)
        v = 1 if t == 16 else 0
        Um, Bm_, NL, NU = self.Umat[v], self.Bmat[v], self.NEGLm[v], self.NEGUm[v]
        A1, A1r = m["A1"]
        Ds, Dsr = m["Ds"]
        DTs, DTsr = m["DTs"]
        DTi, DTir = m["DTi"]
        QK, QKr = O["QK"]
        kd, kdr = O["kd"]
        bv, bvr = m["bv"]
        kegb, kegbr = m["kegb"]
        zw, zwr = O["zw"]
        hT, hrd, wg, wgr = self.hT, L["hrd"], L["wg"], L["wgr"]
        pb, pr = self.bank()
        self.tr(pb[:, 0:128], ck[:, tc], 128, [ckr], [pr])
        self.tr(pb[:, 128:256], cv[:, tc], 128, [cvr], [pr])
        self.ts("dve", kd[:], pb[:, 0:128], L["egrev"][:, t, h:h + 1], None, ALU.mult, None, [pr, L["egrevr"]], [kdr])
        hs, hsr = L["hs"], L["hsr"]
        self.ts("dve", bv[:], pb[:, 128:256], hs[:, 4, t:t + 1], None, ALU.mult, None, [pr, hsr], [bvr])
        self.ts("dve", kegb[:], pb[:, 0:128], hs[:, 5, t:t + 1], None, ALU.mult, None, [pr, hsr], [kegbr])
        yield
        pb, pr = self.bank()
        for k in range(8):
            self.mm(pb[:, 0:128], hT[:, k, tc], wg[:, k, 3, :], k == 0, k == 7, hrd + [wgr], [pr])
        self.act(zw[:], pb[:, 0:128], AF.Silu, [pr], [zwr])
        self.ts("dve", A1[:], Um, L["gtok"][:, t, h:h + 1], None, ALU.mult, None, [cmr, L["gtokr"]], [A1r])
        yield
        self.tt("pool", zw[:], zw[:], self.wonb[:, l * 128:(l + 1) * 128], ALU.mult, [zwr, self.wonbr], [zwr])
        pb, pr = self.bank()
        self.mm(pb[:, 0:128], A1[:], Bm_, True, False, [A1r, cmr], [pr])
        self.mm(pb[:, 0:128], ident, NL, False, True, [cmr], [pr])
        self.act(Ds[:], pb[:, 0:128], AF.Exp, [pr], [Dsr])
        yield
        X, Xr = m["X0"]
        Y, Yr = m["Y0"]
        Pm, Pr = m["P0"]
        pb, pr = self.bank()
        self.mm(pb[:, 0:128], ck[:, tc], ck[:, tc], True, True, [ckr], [pr])
        self.stt("dve", X[:], pb[:, 0:128], hs[:, 3, t:t + 1], Ds[:], ALU.mult, ALU.mult,
                 [pr, Dsr, hsr], [Xr])
        self.tt("pool", DTi[:], Ds[:], ident, ALU.add, [Dsr, cmr], [DTir])
        yield
        pb, pr = self.bank()
        self.tr(pb[:, 0:128], X[:], 128, [Xr], [pr])
        self.cp("act", Y[:], pb[:, 0:128], [pr], [Yr])
        pb, pr = self.bank()
        self.mm(pb[:, 0:128], cq[:, tc], ck[:, tc], True, True, [ckr, cqr], [pr])
        self.stt("dve", DTs[:], pb[:, 0:128], hs[:, 0, t:t + 1], DTi[:], ALU.mult, ALU.mult, [pr, DTir, hsr], [DTsr])
        yield
        pb, pr = self.bank()
        self.tr(pb[:, 0:128], DTs[:], 128, [DTsr], [pr])
        self.cp("act", QK[:], pb[:, 0:128], [pr], [QKr])
        if t == 16:
            self.tt("pool", Pm[:], Y[:], ident, ALU.add, [Yr, cmr], [Pr])
            yield
            cur = 0
            for it in range(2):
                last = it == 1
                nx = 1 - cur
                X2, X2r = m["X%d" % nx]
                Y2, Y2r = m["Y%d" % nx]
                P2, P2r = m["P%d" % nx] if not last else m["W"]
                pb, pr = self.bank()
                self.mm(pb[:, 0:128], Y[:], X[:], True, True, [Yr, Xr], [pr])
                self.cp("act", X2[:], pb[:, 0:128], [pr], [X2r])
                if not last:
                    pb, pr = self.bank()
                    self.mm(pb[:, 0:128], X[:], Y[:], True, True, [Yr, Xr], [pr])
                    self.cp("dve", Y2[:], pb[:, 0:128], [pr], [Y2r])
                yield
                pb, pr = self.bank()
                self.mm(pb[:, 0:128], X2[:], Pm[:], True, True, [X2r, Pr], [pr])
                self.tt("dve", P2[:], pb[:, 0:128], Pm[:], ALU.add, [pr, Pr], [P2r])
                yield
                X, Xr, Y, Yr, Pm, Pr = X2, X2r, Y2, Y2r, P2, P2r
                cur = nx
            yield from self.gdn_solve_tail(m, O)
            return
        XF, XFr = m["XF"]
        YF, YFr = m["YF"]
        self.cp("pool", XF[:], X[:], [Xr], [XFr])
        self.cp("pool", YF[:], Y[:], [Yr], [YFr])
        Tm, Tr = m["T0"]
        self.tt("dve", X[:], X[:], self.bd16, ALU.mult, [Xr, cmr], [Xr])
        self.tt("dve", Y[:], Y[:], self.bd16, ALU.mult, [Yr, cmr], [Yr])
        yield
        self.tt("pool", Pm[:], Y[:], ident, ALU.add, [Yr, cmr], [Pr])
        self.tt("pool", Tm[:], X[:], ident, ALU.add, [Xr, cmr], [Tr])
        cur = 0
        for it in range(3):
            nx = 1 - cur
            X2, X2r = m["X%d" % nx]
            Y2, Y2r = m["Y%d" % nx]
            P2, P2r = m["P%d" % nx]
            T2, T2r = m["T%d" % nx]
            pb, pr = self.bank()
            self.mm(pb[:, 0:128], Y[:], X[:], True, True, [Yr, Xr], [pr])
            self.cp("act", X2[:], pb[:, 0:128], [pr], [X2r])
            pb, pr = self.bank()
            self.mm(pb[:, 0:128], X[:], Y[:], True, True, [Yr, Xr], [pr])
            self.cp("act", Y2[:], pb[:, 0:128], [pr], [Y2r])
            yield
            pb, pr = self.bank()
            self.mm(pb[:, 0:128], X2[:], Pm[:], True, True, [X2r, Pr], [pr])
            self.tt("dve", P2[:], pb[:, 0:128], Pm[:], ALU.add, [pr, Pr], [P2r])
            pb, pr = self.bank()
            self.mm(pb[:, 0:128], Y2[:], Tm[:], True, True, [Y2r, Tr], [pr])
            self.tt("dve", T2[:], pb[:, 0:128], Tm[:], ALU.add, [pr, Tr], [T2r])
            yield
            X, Xr, Y, Yr, Pm, Pr, Tm, Tr = X2, X2r, Y2, Y2r, P2, P2r, T2, T2r
            cur = nx
        Xo, Xor = m["Xo"]
        Yo, Yor = m["Yo"]
        Gs, Gsr = m["Gs"]
        Fs, Fsr = m["Fs"]
        for bsz in (16, 32, 64):
            lastl = bsz == 64
            nx = 1 - cur
            P2, P2r = m["P%d" % nx] if not lastl else m["W"]
            T2, T2r = m["T%d" % nx]
            self.tt("pool", Xo[:], XF[:], self.mo[bsz], ALU.mult, [XFr, cmr], [Xor])
            if not lastl:
                self.tt("pool", Yo[:], YF[:], self.moT[bsz], ALU.mult, [YFr, cmr], [Yor])
            yield
            pb, pr = self.bank()
            self.mm(pb[:, 0:128], Xo[:], Pm[:], True, True, [Xor, Pr], [pr])
            self.cp("act", Gs[:], pb[:, 0:128], [pr], [Gsr])
            if not lastl:
                pb, pr = self.bank()
                self.mm(pb[:, 0:128], Yo[:], Tm[:], True, True, [Yor, Tr], [pr])
                self.cp("act", Fs[:], pb[:, 0:128], [pr], [Fsr])
            yield
            pb, pr = self.bank()
            self.mm(pb[:, 0:128], Tm[:], Gs[:], True, True, [Tr, Gsr], [pr])
            self.tt("dve", P2[:], pb[:, 0:128], Pm[:], ALU.add, [pr, Pr], [P2r])
            if not lastl:
                pb, pr = self.bank()
                self.mm(pb[:, 0:128], Pm[:], Fs[:], True, True, [Pr, Fsr], [pr])
                self.tt("dve", T2[:], pb[:, 0:128], Tm[:], ALU.add, [pr, Tr], [T2r])
            yield
            Pm, Pr, Tm, Tr = P2, P2r, T2, T2r
            cur = nx
        yield from self.gdn_solve_tail(m, O)

    def gdn_solve_tail(self, m, O):
        W, Wr = m["W"]
        bv, bvr = m["bv"]
        kegb, kegbr = m["kegb"]
        U0, U0r = O["U0"]
        WmT, WmTr = O["WmT"]
        pb, pr = self.bank()
        self.mm(pb[:, 0:128], W[:], bv[:], True, True, [Wr, bvr], [pr])
        self.cp("act", U0[:], pb[:, 0:128], [pr], [U0r])
        pb, pr = self.bank()
        self.mm(pb[:, 0:128], kegb[:], W[:], True, True, [Wr, kegbr], [pr])
        self.cp("act", WmT[:], pb[:, 0:128], [pr], [WmTr])
        yield

    def gdn_state(self, l, h, tiles, Os, L):
        cmr = self.cmr
        m = L["mats"]
        (cq, cqr), (ck, ckr), (cv, cvr) = L["cb"]
        Sst, Sstr, Sb, Sbr = L["Sst"], L["Sstr"], L["Sb"], L["Sbr"]
        kTm, kTmr, qTm, qTmr, kdm, kdmr = L["kTm"], L["kTmr"], L["qTm"], L["qTmr"], L["kdm"], L["kdmr"]
        Xp, Xpr = m["Xp"]
        vn, vnr = m["vn"]
        t1, t1r = m["t1"]
        o, orr = m["o"]
        of, ofr = m["of"]
        junk, junkr = m["junk"]
        sm, smr = L["sm"], L["smr"]
        for i, t in enumerate(tiles):
            O = Os[i]
            U0, U0r = O["U0"]
            WmT, WmTr = O["WmT"]
            QK, QKr = O["QK"]
            kd, kdr = O["kd"]
            zw, zwr = O["zw"]
            tc = slice(t * 128, (t + 1) * 128)
            pkS, pkSr = self.bank()
            if t < 16:
                self.mm(pkS[:, 0:128], WmT[:], Sst[:], True, True, [WmTr, Sstr], [pkSr])
            else:
                dg = lambda tns: AP(tns, 0, [[2048, 128], [136, 16], [1, 8]])
                self.cp("pool", dg(kTm), WmT[:].rearrange("p (b i) -> p b i", b=16), [WmTr], [kTmr])
                self.cp("pool", dg(qTm), cq[:, tc].rearrange("p (b i) -> p b i", b=16), [cqr], [qTmr])
                for b in range(16):
                    self.mm(pkS[:, 0:128], kTm[:, b, :], Sb[:, b, :], b == 0, b == 15, [kTmr, Sbr], [pkSr])
            self.tt("dve", vn[:], pkS[:, 0:128], U0[:], ALU.add, [pkSr, U0r], [vnr])
            pqS, pqSr = self.bank()
            if t < 16:
                self.mm(pqS[:, 0:128], cq[:, tc], Sst[:], True, True, [cqr, Sstr], [pqSr])
            else:
                for b in range(16):
                    self.mm(pqS[:, 0:128], qTm[:, b, :], Sb[:, b, :], b == 0, b == 15, [qTmr, Sbr], [pqSr])
            self.act(t1[:], pqS[:, 0:128], AF.Identity, [pqSr, L["hsr"]], [t1r], scale=L["hs"][:, 6, t:t + 1])
            yield
            if t < 16:
                pb, pr = self.bank()
                self.mm(pb[:, 0:128], kd[:], vn[:], True, True, [kdr, vnr], [pr])
                self.stt("dve", Sst[:], Sst[:], L["glb"][:, t * 8 + h:t * 8 + h + 1], pb[:, 0:128], ALU.mult, ALU.add,
                         [Sstr, L["glbr"], pr], [Sstr])
            else:
                self.tt("dve", kdm[:], AP(kd, 0, [[128, 128], [0, 16], [1, 128]]),
                        AP(self.csm, 0, [[64, 128], [1, 16], [0, 128]]), ALU.mult, [kdr, self.csmr], [kdmr])
                for b4 in range(4):
                    pb, pr = self.bank()
                    for bb in range(4):
                        b = b4 * 4 + bb
                        self.mm(pb[:, bb * 128:(bb + 1) * 128], kdm[:, b, :], vn[:], True, True, [kdmr, vnr], [pr])
                    for bb in range(4):
                        b = b4 * 4 + bb
                        self.stt("dve", Sb[:, b, :], Sb[:, b, :], L["glS"][:, h * 16 + b:h * 16 + b + 1],
                                 pb[:, bb * 128:(bb + 1) * 128], ALU.mult, ALU.add, [Sbr, L["glSr"], pr], [Sbr])
            pb, pr = self.bank()
            self.mm(pb[:, 0:128], QK[:], vn[:], True, True, [QKr, vnr], [pr])
            self.tt("dve", o[:], pb[:, 0:128], t1[:], ALU.add, [pr, t1r], [orr])
            yield
            self.act(junk[:], o[:], AF.Square, [orr], [junkr, smr], accum=sm[:, 0:1])
            yield
            self.act(sm[:, 1:2], sm[:, 0:1], AF.Sqrt, [smr], [smr], bias=1e-6, scale=1.0 / 128)
            yield
            self.S.op("dve", lambda e: e.reciprocal(out=sm[:, 2:3], in_=sm[:, 1:2]), rd=[smr], wr=[smr])
            yield
            self.stt("dve", of[:], o[:], sm[:, 2:3], zw[:], ALU.mult, ALU.mult, [orr, smr, zwr], [ofr])
            yield
            pb, pr = self.bank()
            self.tr(pb[:, 0:128], of[:], 128, [ofr], [pr])
            self.cp("act", L["oaT"][:, tc], pb[:, 0:128], [pr], [L["oaTr"]])
            yield

    def swa_phase(self, l):
        S, dr = self.S, self.dr
        hT, hrd, cmr = self.hT, list(self.hTg), self.cmr
        w_in_l = self.w_in.ap()[l].rearrange("(k p) n -> p k n", p=128)
        QB0, KB0, VB0 = 4112, 5136, 5392
        with contextlib.ExitStack() as ph:
            sb = lambda name, shape, dt=F32: self.sb(ph, name, shape, dt)
            wk, wkr = sb("wk", [128, 8, 4, 2, 64], BF16)
            wv, wvr = sb("wv", [128, 8, 4, 2, 64], BF16)
            wstg, wstgr = sb("wstg", [128, 8, 256], BF16)
            for (wdst, wdr, c0) in ((wk, wkr, KB0), (wv, wvr, VB0)):
                self.dma(wstg[:], w_in_l[:, :, c0:c0 + 256], [dr["w_in"]], [wstgr], q="pool")
                for k in range(8):
                    srcv = wstg[:, k, :].rearrange("p (a d) -> p a d", a=4)
                    self.cp("act" if k % 2 else "pool", wdst[:, k, :, 0, :], srcv, [wstgr], [wdr])
                    self.cp("pool" if k % 2 else "act", wdst[:, k, :, 1, :], srcv, [wstgr], [wdr])
            kcT, kcTr = sb("kcT", [128, 16, 4, 128], BF16)
            vcd, vcdr = sb("vcd", [128, 16, 4, 2, 64], BF16)
            for dup in range(2):
                for kp in range(4):
                    self.dma(vcd[:, :, kp, dup, :], self.cv_in.ap()[l][:, :, kp * 64:(kp + 1) * 64].rearrange("b p d -> p b d"),
                             [dr["cv_in"]], [vcdr], q="pool")
            kst, kstr = sb("kst", [128, 4, 4, 2, 64])
            for b4 in range(4):
                for dup in range(2):
                    for kp in range(4):
                        self.dma(kst[:, :, kp, dup, :],
                                 self.ck_in.ap()[l, b4 * 4:(b4 + 1) * 4][:, :, kp * 64:(kp + 1) * 64].rearrange("b p d -> p b d"),
                                 [dr["ck_in"]], [kstr])
                for bb in range(4):
                    pb, pr = self.bank()
                    for kp in range(4):
                        self.tr(pb[:, kp * 128:(kp + 1) * 128], kst[:, bb, kp, :, :].rearrange("p a d -> p (a d)"), 128,
                                [kstr], [pr])
                    self.cp("act", kcT[:, b4 * 4 + bb, :, :], pb[:].rearrange("p (a c) -> p a c", a=4), [pr], [kcTr])
            if self.stop == "w0a":
                S.flush()
                return
            kdT, kdTr = sb("kdT", [128, 4, T], BF16)
            vd, vdr = sb("vd", [128, NT, 512], BF16)
            ktv, ktvr = sb("ktv", [128, 2, 2, 512])
            for kp in range(4):
                for (g0, gw) in GROUPS:
                    pb, pr = self.bank()
                    for k in range(8):
                        self.mm(pb[:, 0:gw], wk[:, k, kp, :, :].rearrange("p a d -> p (a d)"), hT[:, k, g0:g0 + gw],
                                k == 0, k == 7, hrd + [wkr], [pr])
                    self.cp("act", kdT[:, kp, g0:g0 + gw], pb[:, 0:gw], [pr], [kdTr])
            if self.stop == "w0b":
                S.flush()
                return
            for t in range(NT):
                pb, pr = self.bank()
                for k in range(8):
                    self.mm(pb[:], hT[:, k, t * 128:(t + 1) * 128], wv[:, k, :, :, :].rearrange("p a b d -> p (a b d)"),
                            k == 0, k == 7, hrd + [wvr], [pr])
                self.cp("act", vd[:, t, :], pb[:], [pr], [vdr])
                if t >= 15 and self.stop != "w0c":
                    self.cp("dve", ktv[:, 1, t - 15, :], pb[:], [pr], [ktvr])
                    pb, pr = self.bank()
                    for k in range(8):
                        self.mm(pb[:], hT[:, k, t * 128:(t + 1) * 128], wk[:, k, :, :, :].rearrange("p a b d -> p (a b d)"),
                                k == 0, k == 7, hrd + [wkr], [pr])
                    self.cp("dve", ktv[:, 0, t - 15, :], pb[:], [pr], [ktvr])
            if self.stop == "w1":
                S.flush()
                return
            for ci, (pout, sout, cin) in enumerate(((self.p_ck, self.s_ck, self.ck_in), (self.p_cv, self.s_cv, self.cv_in))):
                nm_p, nm_s, nm_i = (("p_ck", "s_ck", "ck_in"), ("p_cv", "s_cv", "cv_in"))[ci]
                src = lambda ti, p0, p1: AP(ktv, ((ci * 2 + ti) * 512), [[2048, 128], [128, 4], [1, 64]])[p0:p1]
                self.dma(pout.ap()[l].rearrange("p (a d) -> p a d", a=4), src(0, 0, 128), [ktvr], [dr[nm_p]])
                self.dma(sout.ap()[l, :, 0:120, :], cin.ap()[l, :, 8:128, :], [dr[nm_i]], [dr[nm_s]])
                for b in range(16):
                    self.dma(sout.ap()[l, b, 120:128, :].rearrange("p (a d) -> p a d", a=4), src(1, 8 * b, 8 * b + 8),
                             [ktvr], [dr[nm_s]])
            if self.stop == "w2":
                S.flush()
                return
            if self.stop == "w3":
                S.flush()
                return
            wq, wqr = sb("wq", [128, 8, 128], BF16)
            qT, qTr = sb("qT", [128, T], BF16)
            obT, obTr = sb("obT", [128, T], BF16)
            Bp, Bpr = sb("Bp", [128, 2, 256])
            Bs, Bsr = sb("Bs", [128, 2, 256])
            qTm, qTmr = sb("qTm2", [128, 16, 128], BF16)
            pTm, pTmr = sb("pTm", [128, 16, 128], BF16)
            bufsets = [(sb("scs%d" % i, [128, 2, 256]), sb("pns%d" % i, [128, 2, 256]), sb("pTs%d" % i, [128, 2, 2, 128], BF16),
                        sb("sms%d" % i, [128, 16])) for i in range(5)]
            negsnk, negsnkr = sb("negsnk", [128, 32])
            self.ts("dve", negsnk[:], self.snk[:], -1.0, None, ALU.mult, None, [self.snkr], [negsnkr])
            if self.debug:
                self.free_probe("swa")
            self.ms("pool", qTm[:], 0.0, [qTmr])
            self.ms("pool", pTm[:], 0.0, [pTmr])
            dg = lambda tns: AP(tns, 0, [[2048, 128], [136, 16], [1, 8]])
            it = 0
            for hp in range(8):
                kp = hp // 2
                self.dma(wq[:], w_in_l[:, :, QB0 + hp * 128:QB0 + (hp + 1) * 128], [dr["w_in"]], [wqr], q="pool")
                self.dma(Bp[:], AP(self.Z, (2 * hp) * 128 * 384 + 127, [[383, 128], [128 * 384, 2], [1, 256]]),
                         [dr["Z"]], [Bpr])
                self.dma(Bs[:], self.ZS.ap()[:, 2 * hp:2 * hp + 2, :], [dr["ZS"]], [Bsr])
                for (g0, gw) in GROUPS:
                    pb, pr = self.bank()
                    for k in range(8):
                        self.mm(pb[:, 0:gw], wq[:, k, :], hT[:, k, g0:g0 + gw], k == 0, k == 7, hrd + [wqr], [pr])
                    self.act(qT[:, g0:g0 + gw], pb[:, 0:gw], AF.Identity, [pr], [qTr], scale=0.125)
                self.cp("pool", dg(qTm), qT[:, TP:T].rearrange("p (b i) -> p b i", b=16), [qTr], [qTmr])
                def swa_iter(t, bufset):
                    tc = slice(t * 128, (t + 1) * 128)
                    (sc, scr_), (pn, pnr), (pT, pTr), (sm, smr) = bufset
                    k0 = 128 if t == 0 else 0
                    ks = slice(k0, 256)
                    bias, biasr = (Bs, Bsr) if t == 16 else (Bp, Bpr)
                    for a in range(2):
                        pb, pr = self.bank()
                        pa = slice(a * 64, (a + 1) * 64)
                        if t == 0:
                            self.mm(pb[:, 128:256], qT[pa, tc], kdT[pa, kp, 0:128], True, True, [qTr, kdTr], [pr])
                        elif t < 16:
                            self.mm(pb[:, 0:256], qT[pa, tc], kdT[pa, kp, (t - 1) * 128:(t + 1) * 128], True, True,
                                    [qTr, kdTr], [pr])
                        else:
                            for b in range(16):
                                self.mm(pb[:, 0:128], qTm[pa, b, :], kcT[pa, b, kp, :], b == 0, b == 15, [qTmr, kcTr], [pr])
                            self.mm(pb[:, 128:256], qT[pa, tc], kdT[pa, kp, tc], True, True, [qTr, kdTr], [pr])
                        self.tt("dve", sc[:, a, ks], pb[:, ks], bias[:, a, ks], ALU.add, [pr, biasr], [scr_])
                    yield
                    if self.stop == "w7":
                        return
                    self.S.op("dve", lambda e, o_=sm[:, 0:2], i_=sc[:, :, ks]: e.reduce_max(out=o_, in_=i_, axis=AX.X),
                              rd=[scr_], wr=[smr])
                    yield
                    if self.stop == "w8":
                        return
                    nsk = negsnk[:, l * 16 + 2 * hp:l * 16 + 2 * hp + 2]
                    self.stt("dve", sm[:, 2:4], sm[:, 0:2], -1.0, nsk, ALU.mult, ALU.min, [smr, negsnkr], [smr])
                    yield
                    for a in range(2):
                        self.act(pn[:, a, ks], sc[:, a, ks], AF.Exp, [scr_, smr], [pnr, smr], bias=sm[:, 2 + a:3 + a],
                                 accum=sm[:, 4 + a:5 + a])
                    self.tt("dve", sm[:, 6:8], sm[:, 2:4], nsk, ALU.subtract, [smr, negsnkr], [smr])
                    yield
                    self.act(sm[:, 6:8], sm[:, 6:8], AF.Exp, [smr], [smr])
                    yield
                    if self.stop == "w9":
                        return
                    self.tt("pool", sm[:, 8:10], sm[:, 4:6], sm[:, 6:8], ALU.add, [smr], [smr])
                    yield
                    self.S.op("dve", lambda e, o_=sm[:, 10:12], i_=sm[:, 8:10]: e.reciprocal(out=o_, in_=i_), rd=[smr], wr=[smr])
                    yield
                    nk = 256 - k0
                    self.tt("pool", pn[:, :, ks], pn[:, :, ks], AP(sm, 10, [[16, 128], [1, 2], [0, nk]]), ALU.mult,
                            [pnr, smr], [pnr])
                    yield
                    if self.stop == "w10":
                        return
                    pb2, pr2 = self.bank()
                    h0 = k0 // 128
                    for a in range(2):
                        for half in range(h0, 2):
                            self.tr(pb2[:, (a * 2 + half) * 128:(a * 2 + half + 1) * 128], pn[:, a, half * 128:(half + 1) * 128],
                                    128, [pnr], [pr2])
                    self.cp("act", pT[:, :, h0:2, :], pb2[:].rearrange("p (a h c) -> p a h c", a=2, h=2)[:, :, h0:2, :], [pr2], [pTr])
                    yield
                    if self.stop == "w11":
                        return
                    pb3, pr3 = self.bank()
                    vown = vd[:, t, kp * 128:(kp + 1) * 128]
                    for a in range(2):
                        oo = slice(a * 128, (a + 1) * 128)
                        if t == 0:
                            self.mm(pb3[:, oo], vown, pT[:, a, 1, :], True, True, [vdr, pTr], [pr3])
                        elif t < 16:
                            self.mm(pb3[:, oo], vd[:, t - 1, kp * 128:(kp + 1) * 128], pT[:, a, 0, :], True, False, [vdr, pTr], [pr3])
                            self.mm(pb3[:, oo], vown, pT[:, a, 1, :], False, True, [vdr, pTr], [pr3])
                        else:
                            self.cp("pool", dg(pTm), pT[:, a, 0, :].rearrange("p (b i) -> p b i", b=16), [pTr], [pTmr])
                            for b in range(16):
                                self.mm(pb3[:, oo], vcd[:, b, kp, :, :].rearrange("p a d -> p (a d)"), pTm[:, b, :],
                                        b == 0, False, [vcdr, pTmr], [pr3])
                            self.mm(pb3[:, oo], vown, pT[:, a, 1, :], False, True, [vdr, pTr], [pr3])
                    for a in range(2):
                        pa = slice(a * 64, (a + 1) * 64)
                        self.cp("act", obT[pa, tc], pb3[pa, a * 128:(a + 1) * 128], [pr3], [obTr])
                    yield

                NW = 5
                pending = list(range(NT))
                active = []
                freeb = list(range(NW))
                while pending or active:
                    while pending and freeb:
                        j = freeb.pop(0)
                        active.append((swa_iter(pending.pop(0), bufsets[j]), j))
                    for g_ in list(active):
                        try:
                            next(g_[0])
                        except StopIteration:
                            active.remove(g_)
                            freeb.append(g_[1])
                self.dma(self.ob.ap()[hp], obT[:], [obTr], [dr["ob"]])
            S.flush()

    def layer_norm(self, r, rr, tw, gi, bi, l, L, fuse=None):
        cmr = self.cmr
        mean, meanr = L["mean"], L["meanr"]
        rstd, rstdr = L["rstd"], L["rstdr"]
        pM, pMr = self.bank()
        for k in range(8):
            self.mm(pM[:, 0:tw], self.onesm, r[:, k, 0:tw], k == 0, k == 7, [cmr, rr], [pMr])
        pQ, pQr = self.bank()
        for k in range(8):
            sq, sqr = L["sq2"][k % 2]
            self.act(sq[:, 0:tw], r[:, k, 0:tw], AF.Square, [rr], [sqr])
            self.mm(pQ[:, 0:tw], self.onesm, sq[:, 0:tw], k == 0, k == 7, [cmr, sqr], [pQr])
        self.cp("act", mean[:, 0:tw], pM[:, 0:tw], [pMr], [meanr])
        self.act(rstd[:, 0:tw], pM[:, 0:tw], AF.Square, [pMr], [rstdr])
        self.tt("dve", rstd[:, 0:tw], pQ[:, 0:tw], rstd[:, 0:tw], ALU.subtract, [pQr, rstdr], [rstdr])
        self.act(rstd[:, 0:tw], rstd[:, 0:tw], AF.Sqrt, [rstdr], [rstdr], bias=1e-5)
        self.S.op("dve", lambda e, a=rstd[:, 0:tw]: e.reciprocal(out=a, in_=a), rd=[rstdr], wr=[rstdr])
        if fuse is None:
            bc = lambda tns: AP(tns, 0, [[512, 128], [0, 8], [1, tw]])
            self.tt("dve", r[:, :, 0:tw], r[:, :, 0:tw], bc(mean), ALU.subtract, [rr, meanr], [rr])
            self.tt("dve", r[:, :, 0:tw], r[:, :, 0:tw], bc(rstd), ALU.mult, [rr, rstdr], [rr])
            for k in range(8):
                self.ts("dve", r[:, k, 0:tw], r[:, k, 0:tw], self.lnp[:, k, gi * 2 + l:gi * 2 + l + 1],
                        self.lnp[:, k, bi * 2 + l:bi * 2 + l + 1], ALU.mult, ALU.add, [rr, self.lnpr], [rr])
            return
        dstf, dstr, fs, fsr, frow = fuse
        cks = [Reg() for _ in range(8)]
        for k in range(8):
            self.tt("dve", r[:, k, 0:tw], r[:, k, 0:tw], mean[:, 0:tw], ALU.subtract, [rr, meanr], [cks[k]])
            self.tt("dve", r[:, k, 0:tw], r[:, k, 0:tw], rstd[:, 0:tw], ALU.mult, [cks[k], rstdr], [cks[k]])
        for k in range(8):
            if dstf is not None:
                self.act(dstf(k), r[:, k, 0:tw], AF.Identity, [cks[k], fsr], [dstr], scale=fs[:, frow, k:k + 1],
                         bias=fs[:, frow + 1, k:k + 1])
            self.act(r[:, k, 0:tw], r[:, k, 0:tw], AF.Identity, [cks[k], self.lnpr], [cks[k]] + ([rr] if k == 7 else []),
                     scale=self.lnp[:, k, gi * 2 + l:gi * 2 + l + 1], bias=self.lnp[:, k, bi * 2 + l:bi * 2 + l + 1])

    def gated_resid(self, l, gchunk0, pY, pYr, m, t0, tw, xt, xtr, r, rr, L):
        modT, modr = self.modT, self.modr
        if t0 < TP:
            self.act(r[:, m, 0:tw], pY[:, 0:tw], AF.Identity, [pYr, modr], [rr], scale=modT[:, l, gchunk0 + m, 0:1])
        else:
            gb = AP(modT, l * 48 * 17 + (gchunk0 + m) * 17 + 1, [[2 * 48 * 17, 128], [1, 16], [0, 8]])
            self.tt("dve", r[:, m, 0:128].rearrange("p (b i) -> p b i", b=16),
                    pY[:, 0:128].rearrange("p (b i) -> p b i", b=16), gb, ALU.mult, [pYr, modr], [rr])
        self.stt("dve", r[:, m, 0:tw], xt[:, m, 0:tw], ALPHA, r[:, m, 0:tw], ALU.mult, ALU.add, [xtr, rr], [rr])

    def mlp_phase(self, l):
        S, dr = self.S, self.dr
        hT, hTg, cmr = self.hT, self.hTg, self.cmr
        w_in_l = self.w_in.ap()[l].rearrange("(k p) n -> p k n", p=128)
        GA0, GB0 = 5648, 6672
        wsrc = lambda w: w.ap()[l].rearrange("(k p) n -> p k n", p=128)
        with contextlib.ExitStack() as ph:
            sb = lambda name, shape, dt=F32: self.sb(ph, name, shape, dt)
            wts = [sb("wt%d" % i, [128, 8, 512], BF16) for i in range(4)]
            self.wi = 0

            pidx = {p_[0]: i_ for i_, p_ in enumerate(self.mlp_pieces(l))}

            def wload(key):
                wt, wr_ = wts[self.wi % 4]
                self.wi += 1
                pi = pidx[key]
                self.dma(wt[:], self.wsc.ap()[l, pi], [self.wscr[l][pi]], [wr_], q="sp")
                return wt, wr_
            oat, oatr = sb("oat", [128, 8, 512], BF16)
            obt, obtr = sb("obt", [128, 8, 512], BF16)
            xt, xtr = sb("xt", [128, 8, 512])
            r, rr = sb("r", [128, 8, 512])
            mixed, mixedr = sb("mixed", [128, 8, 512], BF16)
            h2, h2r = sb("h2", [128, 8, 512], BF16)
            upT, upTr = sb("upT", [128, 32, 512], BF16)
            sg, sgr = sb("sg", [128, 512])
            mA, mAr = sb("mA", [128, 512])
            mB, mBr = sb("mB", [128, 512])
            ur, urr = sb("ur", [128, 512])
            L = {}
            L["sq2"] = [(sg, sgr), (ur, urr)]
            L["mean"], L["meanr"] = sb("mean", [128, 512])
            L["rstd"], L["rstdr"] = sb("rstd", [128, 512])
            yst1 = sb("yst", [128, D])
            yst = [(yst1[0][:], yst1[1])] * 2
            self.rot = [4, 5, 6, 7]
            fs, fsr = sb("fs", [128, 4, 8])
            lnp, lnpr, modT, modr = self.lnp, self.lnpr, self.modT, self.modr
            mcol = lambda ll, c0: AP(modT, ll * 48 * 17 + c0 * 17, [[2 * 48 * 17, 128], [17, 8]])
            lrow = lambda rw: AP(lnp, rw, [[64, 128], [8, 8]])
            self.tt("dve", fs[:, 0, :], lrow(0 * 2 + l), mcol(l, 32), ALU.mult, [lnpr, modr], [fsr])
            self.tt("dve", fs[:, 1, :], lrow(1 * 2 + l), mcol(l, 32), ALU.mult, [lnpr, modr], [fsr])
            self.tt("dve", fs[:, 1, :], fs[:, 1, :], mcol(l, 24), ALU.add, [fsr, modr], [fsr])
            if l == 0:
                self.tt("dve", fs[:, 2, :], lrow(2 * 2 + l), mcol(1, 8), ALU.mult, [lnpr, modr], [fsr])
                self.tt("dve", fs[:, 3, :], lrow(3 * 2 + l), mcol(1, 8), ALU.mult, [lnpr, modr], [fsr])
                self.tt("dve", fs[:, 3, :], fs[:, 3, :], mcol(1, 0), ALU.add, [fsr, modr], [fsr])
            if self.debug:
                self.free_probe("mlp")
            def st_merge(gi):
                t0, tw = GROUPS[gi]
                hr = hTg[gi]
                self.dma(oat[:, :, 0:tw], self.oa.ap()[:, :, t0:t0 + tw].rearrange("h p t -> p h t"), [dr["oa"]], [oatr])
                self.dma(obt[:, :, 0:tw], self.ob.ap()[:, :, t0:t0 + tw].rearrange("h p t -> p h t"), [dr["ob"]], [obtr])
                hold = [(mA, mAr), (mB, mBr), (L["mean"], L["meanr"]), (L["rstd"], L["rstdr"])]
                for mh in range(2):
                    for br in range(2):
                        if br == 0:
                            wp_t, wp_r = wload(("pa", mh))
                            wg_t, wg_r = wload(("ga", mh))
                            src, srcr = oat, oatr
                        else:
                            wp_t, wp_r = wload(("pb", mh))
                            wg_t, wg_r = wload(("gb", mh))
                            src, srcr = obt, obtr
                        for mmi in range(4):
                            m = mh * 4 + mmi
                            ms_ = slice(mmi * 128, (mmi + 1) * 128)
                            pP, pPr = self.bank()
                            for k in range(8):
                                self.mm(pP[:, 0:tw], wp_t[:, k, ms_], src[:, k, 0:tw], k == 0, k == 7, [wp_r, srcr], [pPr])
                            pG, pGr = self.bank()
                            for k in range(8):
                                self.mm(pG[:, 0:tw], wg_t[:, k, ms_], hT[:, k, t0:t0 + tw], k == 0, k == 7, [wg_r, hr], [pGr])
                            self.act(sg[:, 0:tw], pG[:, 0:tw], AF.Sigmoid, [pGr], [sgr])
                            hd, hdr = hold[mmi]
                            if br == 0:
                                self.tt("dve", hd[:, 0:tw], pP[:, 0:tw], sg[:, 0:tw], ALU.mult, [pPr, sgr], [hdr])
                            else:
                                self.tt("dve", ur[:, 0:tw], pP[:, 0:tw], sg[:, 0:tw], ALU.mult, [pPr, sgr], [urr])
                                self.tt("pool", mixed[:, m, 0:tw], hd[:, 0:tw], ur[:, 0:tw], ALU.add, [hdr, urr], [mixedr])

            def st_rest(gi):
                t0, tw = GROUPS[gi]
                hr = hTg[gi]
                self.dma(xt[:, :, 0:tw], self.xs.ap()[:, :, t0:t0 + tw], [dr["xs"]], [xtr])
                for mh in range(2):
                    wo_t, wo_r = wload(("wo", mh))
                    for mmi in range(4):
                        m = mh * 4 + mmi
                        pY, pYr = self.bank()
                        for k in range(8):
                            self.mm(pY[:, 0:tw], wo_t[:, k, mmi * 128:(mmi + 1) * 128], mixed[:, k, 0:tw], k == 0, k == 7,
                                    [wo_r, mixedr], [pYr])
                        self.gated_resid(l, 16, pY, pYr, m, t0, tw, xt, xtr, r, rr, L)
                if t0 < TP:
                    self.layer_norm(r, rr, tw, 0, 1, l, L, fuse=(lambda k: h2[:, k, 0:tw], h2r, fs, fsr, 0))
                else:
                    self.layer_norm(r, rr, tw, 0, 1, l, L)
                    self.modulate(l, 1, r, rr, t0, tw, _Off(h2, t0), h2r, self.modT, self.modr, ph)
                for fg in range(8):
                    wu_t, wu_r = wload(("wu", fg))
                    for ff in range(4):
                        pU, pUr = self.bank()
                        for k in range(8):
                            self.mm(pU[:, 0:tw], wu_t[:, k, ff * 128:(ff + 1) * 128], h2[:, k, 0:tw], k == 0, k == 7,
                                    [wu_r, h2r], [pUr])
                        self.act(ur[:, 0:tw], pU[:, 0:tw], AF.Relu, [pUr], [urr])
                        self.tt("pool", upT[:, fg * 4 + ff, 0:tw], ur[:, 0:tw], ur[:, 0:tw], ALU.mult, [urr], [upTr])
                wdv = self.w_down.ap()[l].rearrange("(fc p) n -> p fc n", p=128)
                for mh in range(2):
                    for fg in range(4):
                        wd_t, wd_r = wload(("wd", mh, fg))
                        for fc in range(8):
                            f = fg * 8 + fc
                            for mmi in range(4):
                                pD, pDr = self.bank(mmi)
                                self.mm(pD[:, 0:tw], wd_t[:, fc, mmi * 128:(mmi + 1) * 128], upT[:, f, 0:tw], f == 0, f == 31,
                                        [wd_r, upTr], [pDr])
                    for mmi in range(4):
                        pD, pDr = self.bank(mmi)
                        self.gated_resid(l, 40, pD, pDr, mh * 4 + mmi, t0, tw, r, rr, xt, xtr, L)

            def st_ln2(gi):
                t0, tw = GROUPS[gi]
                hr = hTg[gi]
                if t0 < TP and l == 0:
                    self.layer_norm(xt, xtr, tw, 2, 3, l, L, fuse=(lambda k: hT[:, k, t0:t0 + tw], hr, fs, fsr, 2))
                elif t0 < TP:
                    self.layer_norm(xt, xtr, tw, 2, 3, l, L, fuse=(None, None, fs, fsr, 2))
                else:
                    self.layer_norm(xt, xtr, tw, 2, 3, l, L)
                if l == 0:
                    self.dma(self.xs.ap()[:, :, t0:t0 + tw], xt[:, :, 0:tw], [xtr], [dr["xs"]])
                    if t0 >= TP:
                        self.modulate(1, 0, xt, xtr, t0, tw, hT, hr, self.modT, self.modr, None)
                else:
                    for sub in range(tw // 128):
                        ys, ysr = yst[sub % 2]
                        for half in range(2):
                            pb, pr = self.bank()
                            for kk in range(4):
                                k = half * 4 + kk
                                self.tr(pb[:, kk * 128:(kk + 1) * 128], xt[:, k, sub * 128:(sub + 1) * 128], 128, [xtr], [pr])
                            self.cp("act", ys[:, half * 512:(half + 1) * 512], pb[:], [pr], [ysr])
                        self.dma(self.y_tok.ap()[t0 + sub * 128:t0 + (sub + 1) * 128, :], ys, [ysr], [dr["y_tok"]])

            st_merge(0)
            for gi in range(len(GROUPS)):
                st_rest(gi)
                if gi + 1 < len(GROUPS):
                    st_merge(gi + 1)
                st_ln2(gi)
            self.rot = list(range(8))
            S.flush()


class _Off:
    def __init__(self, t, t0):
        self.t, self.t0 = t, t0

    def __getitem__(self, idx):
        p, k, sl = idx
        return self.t[p, k, sl.start - self.t0:sl.stop - self.t0]


def _consts():
    i = np.arange(128)
    blk = i // 8
    c = np.zeros((16, 128, 128), np.float32)
    c[0] = np.eye(128)
    c[1] = 1.0
    c[2] = (i[:, None] <= i[None, :])
    c[3] = (i[:, None] > i[None, :])
    same = blk[:, None] == blk[None, :]
    c[4] = c[2] * same
    c[5] = c[3] * same
    c[6] = np.where(i[:, None] > i[None, :], 0.0, -1e4)
    c[7] = np.where((i[:, None] > i[None, :]) & same, 0.0, -1e4)
    c[8] = 1.0 / 1024
    c[9] = (i[:, None] // 16 == i[None, :] // 16)
    for n_, b_ in enumerate((16, 32, 64)):
        m_ = ((i[:, None] // (2 * b_) == i[None, :] // (2 * b_)) & (i[:, None] % (2 * b_) >= b_) & (i[None, :] % (2 * b_) < b_))
        c[10 + n_] = m_
        c[13 + n_] = m_.T
    small = np.zeros((128, 64), np.float32)
    small[i, blk] = 1.0
    for b in range(16):
        for j in range(3):
            small[8 * b + 5 + j, 16 + 3 * b + j] = 1.0
    sel = np.zeros((8, 8, 128), np.float32)
    for h in range(8):
        sel[h, h, :] = 1.0
    def bucket(n):
        n = max(n, 0)
        if n < 16:
            return n
        v = 16 + int(np.float32(np.log(np.float32(max(n, 16)) / np.float32(16)) / np.float32(math.log(128 / 16)) * np.float32(16)))
        return min(v, 31)
    oh = np.zeros((32, 384), np.float32)
    neg = np.full((16, 384), NEG, np.float32)
    for mpos in range(128, 256):
        oh[bucket(255 - mpos), mpos] = 1.0
        neg[:, mpos] = 0.0
    return c, small, sel.reshape(8, 1024), oh, neg


_NC = None


def kernel(x_prompt, x_sample, state_delta, state_conv, cache_k, cache_v, c_prompt, c_sample,
           rel_bias, w_ada, b_ada, w_in, w_conv, a_log, dt_bias, w_onorm, sinks,
           w_pa, w_pb, w_out, ln1_g, ln1_b, w_up, w_down, ln2_g, ln2_b):
    global _NC
    f = lambda a: np.ascontiguousarray(np.asarray(a, dtype=np.float32))
    if _NC is None:
        _NC = K().build()
    nc = _NC
    cm, small, sel, oh, neg = _consts()
    shared = {
        "rel_bias": f(rel_bias), "w_ada": f(w_ada), "b_ada": f(b_ada), "w_in": f(w_in),
        "w_conv": f(w_conv).reshape(8, 3072), "a_log": f(a_log).reshape(1, 16), "dt_bias": f(dt_bias).reshape(1, 16),
        "w_onorm": f(w_onorm).reshape(1, 256), "sinks": f(sinks).reshape(1, 32),
        "w_pa": f(w_pa), "w_pb": f(w_pb), "w_out": f(w_out),
        "ln_all": f(np.stack([f(ln1_g), f(ln1_b), f(ln2_g), f(ln2_b)], 0)).reshape(8, 1024),
        "w_up": f(w_up), "w_down": f(w_down),
        "cmat": cm, "csmall": small, "csel": sel, "coh": oh, "cneg": neg,
    }
    x_prompt, x_sample = f(x_prompt), f(x_sample)
    state_delta, state_conv, cache_k, cache_v = f(state_delta), f(state_conv), f(cache_k), f(cache_v)
    c_prompt, c_sample = f(c_prompt), f(c_sample)
    in_maps = []
    for c in range(NCORES):
        bs = slice(16 * c, 16 * c + 16)
        m = dict(shared)
        m["x_tok"] = np.ascontiguousarray(np.concatenate([x_prompt[c], x_sample[bs].reshape(128, D)], 0))
        m["c_all"] = np.ascontiguousarray(np.concatenate([c_prompt[c:c + 1], c_sample[bs]], 0))
        m["sd_in"] = np.ascontiguousarray(state_delta[:, bs])
        m["sc_in"] = np.ascontiguousarray(state_conv[:, bs].reshape(2, 48, 3072))
        m["ck_in"] = np.ascontiguousarray(cache_k[:, bs].reshape(2, 16, 128, 256))
        m["cv_in"] = np.ascontiguousarray(cache_v[:, bs].reshape(2, 16, 128, 256))
        in_maps.append(m)
    res = run_bass_kernel_spmd(nc, in_maps, core_ids=list(range(NCORES)))
    R = res.results
    yp = np.stack([R[c]["y_tok"][:TP] for c in range(NCORES)], 0)
    ys = np.concatenate([R[c]["y_tok"][TP:].reshape(16, 8, D) for c in range(NCORES)], 0)
    p_sd = np.stack([R[c]["p_sd"] for c in range(NCORES)], 1)
    p_sc = np.stack([R[c]["p_sc"] for c in range(NCORES)], 1)
    p_ck = np.stack([R[c]["p_ck"].reshape(2, 128, 4, 64) for c in range(NCORES)], 1)
    p_cv = np.stack([R[c]["p_cv"].reshape(2, 128, 4, 64) for c in range(NCORES)], 1)
    s_sd = np.concatenate([R[c]["s_sd"] for c in range(NCORES)], 1)
    s_sc = np.concatenate([R[c]["s_sc"].reshape(2, 16, 3, 3072) for c in range(NCORES)], 1)
    s_ck = np.concatenate([R[c]["s_ck"].reshape(2, 16, 128, 4, 64) for c in range(NCORES)], 1)
    s_cv = np.concatenate([R[c]["s_cv"].reshape(2, 16, 128, 4, 64) for c in range(NCORES)], 1)
    return tuple(np.ascontiguousarray(a.astype(np.float32)) for a in (yp, ys, p_sd, p_sc, p_ck, p_cv, s_sd, s_sc, s_ck, s_cv))
```

```python
import contextlib
import math
import numpy as np
import concourse.bass as bass
import concourse.mybir as mybir
from concourse.bass_utils import run_bass_kernel_spmd

F32 = mybir.dt.float32
BF16 = mybir.dt.bfloat16
AF = mybir.ActivationFunctionType
ALU = mybir.AluOpType
AX = mybir.AxisListType

NCORES = 8
D = 1024
T = 2176
NT = 17
TP = 2048
NIN = 7696
ALPHA = 4 ** 0.25
NEG = -30000.0
GROUPS = [(0, 512), (512, 512), (1024, 512), (1536, 512), (2048, 128)]


class Reg:
    __slots__ = ("w", "r", "excl")

    def __init__(self, excl=False):
        self.w = None
        self.r = {}
        self.excl = excl


class Sched:
    ENG = ("pe", "act", "dve", "pool", "sp")
    NSLOT = {"sp": 8, "pool": 8}

    def __init__(self, nc, st):
        self.nc = nc
        self.ops = {e: [] for e in self.ENG}
        self.cnt = {e: 0 for e in self.ENG}
        self.known = {e: {} for e in self.ENG}
        self.slot_next = {q: 0 for q in self.NSLOT}
        self.slot_cnt = {}
        self.sem = {}
        for e in self.ENG:
            self.sem[e] = st.enter_context(nc.semaphore("s_" + e))
        for q, n in self.NSLOT.items():
            for s in range(n):
                self.sem[("dma", q, s)] = st.enter_context(nc.semaphore("d_%s_%d" % (q, s)))
        self.nblocks = 0

    def _deps(self, eng, rd, wr):
        deps = {}

        def add(ev, same_ok):
            if ev is None:
                return
            k, c = ev
            if k == eng and not same_ok:
                return
            if c > deps.get(k, 0):
                deps[k] = c
        for r in rd:
            add(r.w, eng != "pe")
            if r.excl:
                for k, c in r.r.items():
                    add((k, c), False)
        for r in wr:
            add(r.w, eng != "pe")
            for k, c in r.r.items():
                add((k, c), eng != "pe")
        waits = []
        kn = self.known[eng]
        for k, c in deps.items():
            if c > kn.get(k, 0):
                kn[k] = c
                waits.append((k, c))
        return waits

    def _mark(self, ev, rd, wr):
        k, c = ev
        for r in rd:
            if c > r.r.get(k, 0):
                r.r[k] = c
        for r in wr:
            r.w = ev
            r.r = {}

    def op(self, eng, fn, rd=(), wr=()):
        waits = self._deps(eng, rd, wr)
        self.cnt[eng] += 1
        ev = (eng, self.cnt[eng])
        self.ops[eng].append((waits, fn, (eng, 1)))
        self._mark(ev, rd, wr)

    def dma(self, q, out, in_, rd=(), wr=()):
        waits = self._deps(q, rd, wr)
        s = self.slot_next[q]
        self.slot_next[q] = (s + 1) % self.NSLOT[q]
        key = ("dma", q, s)
        prev = self.slot_cnt.get(key, 0)
        kn = self.known[q]
        if prev > kn.get(key, 0):
            kn[key] = prev
            waits.append((key, prev))
        newc = prev + 16
        self.slot_cnt[key] = newc
        self.ops[q].append((waits, lambda e: e.dma_start(out=out, in_=in_), (key, 16)))
        self._mark((key, newc), rd, wr)

    def barrier(self):
        for e in self.ENG:
            waits = []
            kn = self.known[e]
            for key, c in self.slot_cnt.items():
                if c > kn.get(key, 0):
                    kn[key] = c
                    waits.append((key, c))
            for e2 in self.ENG:
                if e2 != e and self.cnt[e2] > kn.get(e2, 0):
                    kn[e2] = self.cnt[e2]
                    waits.append((e2, self.cnt[e2]))
            if waits:
                self.ops[e].append((waits, None, None))

    def flush(self):
        if not any(self.ops[e] for e in self.ENG):
            return
        self.barrier()
        nc = self.nc
        sem = self.sem
        ops = self.ops
        self.ops = {e: [] for e in self.ENG}
        self.nblocks += 1

        def run(e, lst):
            for waits, fn, inc in lst:
                for k, c in waits:
                    e.wait_ge(sem[k], c)
                if fn is not None:
                    fn(e).then_inc(sem[inc[0]], inc[1])
        with nc.Block() as block:
            @block.tensor
            def _(e):
                run(e, ops["pe"])

            @block.scalar
            def _(e):
                run(e, ops["act"])

            @block.vector
            def _(e):
                run(e, ops["dve"])

            @block.gpsimd
            def _(e):
                run(e, ops["pool"])

            @block.sync
            def _(e):
                run(e, ops["sp"])


def AP(t, off, dims):
    return bass.AP(t, off, [list(d) for d in dims])


class K:
    def __init__(self, stop=None, debug=False):
        self.stop = stop
        self.debug = debug
        self.nc = nc = bass.Bass("TRN2", target_bir_lowering=False)
        self.st = st = contextlib.ExitStack()
        self.S = Sched(nc, st)
        self.dr = {}
        self.ps = []
        self.psr = []
        self.ps_i = 0

    def din(self, name, shape, dt=F32):
        t = self.nc.dram_tensor(name, list(shape), dt, kind="ExternalInput")
        self.dr[name] = Reg()
        return t

    def dout(self, name, shape, dt=F32):
        t = self.nc.dram_tensor(name, list(shape), dt, kind="ExternalOutput")
        self.dr[name] = Reg()
        return t

    def dscr(self, name, shape, dt=F32):
        t = self.nc.dram_tensor(name, list(shape), dt, kind="ExternalOutput" if self.debug else "Internal")
        self.dr[name] = Reg()
        return t

    def sb(self, stack, name, shape, dt=F32):
        self.nsb = getattr(self, "nsb", 0) + 1
        t = stack.enter_context(self.nc.sbuf_tensor("%s_%d" % (name, self.nsb), list(shape), dt))
        return t, Reg()

    def free_probe(self, tag):
        for kb in range(200, 0, -2):
            try:
                with self.nc.sbuf_tensor("probe_%s_%d" % (tag, kb), [128, kb * 256], F32):
                    pass
                print("SBUF free at", tag, ":", kb, "KB")
                return
            except BaseException:
                continue
        print("SBUF free at", tag, ": <2KB")

    def bank(self, i=None):
        if i is None:
            i = self.rot[self.ps_i % len(self.rot)]
            self.ps_i += 1
        return self.ps[i], self.psr[i]

    def mm(self, out, lhsT, rhs, start, stop, rd, wr):
        self.S.op("pe", lambda e: e.matmul(out, lhsT=lhsT, rhs=rhs, start=start, stop=stop), rd=rd, wr=wr)

    def tr(self, out, in_, n, rd, wr):
        idn = self.ident[0:n, 0:n]
        self.S.op("pe", lambda e: e.transpose(out=out, in_=in_, identity=idn), rd=list(rd) + [self.identr], wr=wr)

    def act(self, out, in_, func, rd, wr, bias=0.0, scale=1.0, accum=None):
        if accum is None:
            self.S.op("act", lambda e: e.activation(out=out, in_=in_, func=func, bias=bias, scale=scale), rd=rd, wr=wr)
        else:
            self.S.op("act", lambda e: e.activation(out=out, in_=in_, func=func, bias=bias, scale=scale,
                                                    accum_out=accum), rd=rd, wr=wr)

    def tt(self, eng, out, in0, in1, op, rd, wr):
        self.S.op(eng, lambda e: e.tensor_tensor(out=out, in0=in0, in1=in1, op=op), rd=rd, wr=wr)

    def ts(self, eng, out, in0, s1, s2, op0, op1, rd, wr):
        if s2 is None:
            self.S.op(eng, lambda e: e.tensor_scalar(out=out, in0=in0, scalar1=s1, scalar2=None, op0=op0), rd=rd, wr=wr)
        else:
            self.S.op(eng, lambda e: e.tensor_scalar(out=out, in0=in0, scalar1=s1, scalar2=s2, op0=op0, op1=op1),
                      rd=rd, wr=wr)

    def stt(self, eng, out, in0, sc, in1, op0, op1, rd, wr):
        self.S.op(eng, lambda e: e.scalar_tensor_tensor(out=out, in0=in0, scalar=sc, in1=in1, op0=op0, op1=op1),
                  rd=rd, wr=wr)

    def cp(self, eng, out, in_, rd, wr):
        if eng == "act":
            self.S.op("act", lambda e: e.copy(out=out, in_=in_), rd=rd, wr=wr)
        else:
            self.S.op(eng, lambda e: e.tensor_copy(out=out, in_=in_), rd=rd, wr=wr)

    def ms(self, eng, out, val, wr):
        self.S.op(eng, lambda e: e.memset(out, val), wr=wr)

    def dma(self, out, in_, rd, wr, q="sp"):
        self.S.dma(q, out, in_, rd=rd, wr=wr)

    def rows_T(self, stack, src_ap, srcreg, R, C, dst, dstreg, name):
        rowt, rowr = self.sb(stack, name, [R, C])
        self.dma(rowt[:], src_ap, [srcreg], [rowr])
        nch = C // 128
        per = 512 // R
        c = 0
        while c < nch:
            n = min(per, nch - c)
            pb, pr = self.bank()
            for i in range(n):
                self.tr(pb[:, i * R:(i + 1) * R], rowt[:, (c + i) * 128:(c + i + 1) * 128], R, [rowr], [pr])
            self.cp("dve", dst[:, c:c + n, :], pb[:, 0:n * R].rearrange("p (a b) -> p a b", a=n), [pr], [dstreg])
            c += n

    def build(self):
        nc, S, st = self.nc, self.S, self.st
        x_tok = self.din("x_tok", [T, D])
        c_all = self.din("c_all", [17, D])
        sd_in = self.din("sd_in", [2, 16, 8, 128, 128])
        sc_in = self.din("sc_in", [2, 48, 3072])
        ck_in = self.din("ck_in", [2, 16, 128, 256])
        cv_in = self.din("cv_in", [2, 16, 128, 256])
        rel_bias = self.din("rel_bias", [32, 16])
        w_ada = self.din("w_ada", [2, D, 6144])
        b_ada = self.din("b_ada", [2, 6144])
        w_in = self.din("w_in", [2, D, NIN])
        w_conv = self.din("w_conv", [8, 3072])
        a_log = self.din("a_log", [1, 16])
        dt_bias = self.din("dt_bias", [1, 16])
        w_onorm = self.din("w_onorm", [1, 256])
        sinks = self.din("sinks", [1, 32])
        w_pa = self.din("w_pa", [2, D, D])
        w_pb = self.din("w_pb", [2, D, D])
        w_out = self.din("w_out", [2, D, D])
        ln_all = self.din("ln_all", [8, D])
        w_up = self.din("w_up", [2, D, 4096])
        w_down = self.din("w_down", [2, 4096, D])
        cmat = self.din("cmat", [16, 128, 128])
        csmall = self.din("csmall", [128, 16 + 48])
        csel = self.din("csel", [8, 8 * 128])
        coh = self.din("coh", [32, 384])
        cneg = self.din("cneg", [16, 384])

        y_tok = self.dout("y_tok", [T, D])
        p_sd = self.dout("p_sd", [2, 8, 128, 128])
        p_sc = self.dout("p_sc", [2, 3, 3072])
        p_ck = self.dout("p_ck", [2, 128, 256])
        p_cv = self.dout("p_cv", [2, 128, 256])
        s_sd = self.dout("s_sd", [2, 16, 8, 128, 128])
        s_sc = self.dout("s_sc", [2, 48, 3072])
        s_ck = self.dout("s_ck", [2, 16, 128, 256])
        s_cv = self.dout("s_cv", [2, 16, 128, 256])

        xs = self.dscr("xs", [128, 8, T])
        oa = self.dscr("oa", [8, 128, T], BF16)
        ob = self.dscr("ob", [8, 128, T], BF16)
        Z = self.dscr("Z", [16, 128, 384])
        ZS = self.dscr("ZS", [128, 16, 256])
        wsc = self.dscr("wsc", [2, 26, 128, 8, 512], BF16)
        self.wscr = [[Reg() for _ in range(26)] for _ in range(2)]
        dr = self.dr

        for i in range(8):
            t = st.enter_context(nc.psum_tensor("pb%d" % i, [128, 512], F32))
            self.ps.append(t)
            self.psr.append(Reg(excl=True))
        self.rot = list(range(8))

        P = contextlib.ExitStack()
        st.enter_context(P)
        cm, cmr = self.sb(P, "cm", [128, 16, 128])
        self.ident = cm[:, 0, :]
        self.identr = cmr
        ident = cm[:, 0, :]
        ones = cm[:, 1, :]
        Umat = {0: cm[:, 2, :], 1: cm[:, 4, :]}
        Bmat = {0: cm[:, 3, :], 1: cm[:, 5, :]}
        NEGLm = {0: cm[:, 6, :], 1: cm[:, 7, :]}
        onesm = cm[:, 8, :]
        bd16 = cm[:, 9, :]
        mo = {16: cm[:, 10, :], 32: cm[:, 11, :], 64: cm[:, 12, :]}
        moT = {16: cm[:, 13, :], 32: cm[:, 14, :], 64: cm[:, 15, :]}
        NEGUm = {}
        negu, negur = self.sb(P, "negu", [128, 2, 128])
        NEGUm[0] = negu[:, 0, :]
        NEGUm[1] = negu[:, 1, :]
        csm, csmr = self.sb(P, "csm", [128, 64])
        blkm = csm[:, 0:16]
        selg = csm[:, 16:64]
        sel, selr = self.sb(P, "sel", [8, 8 * 128])
        hT, hTr = self.sb(P, "hT", [128, 8, T], BF16)
        hTg = [Reg() for _ in range(5)]
        modT, modr = self.sb(P, "modT", [128, 2, 48, 17])
        lnp, lnpr = self.sb(P, "lnp", [128, 8, 8])
        wcv, wcvr = self.sb(P, "wcv", [128, 24, 8])
        dtb, dtbr = self.sb(P, "dtb", [128, 16])
        nA, nAr = self.sb(P, "nA", [128, 16])
        wonb, wonbr = self.sb(P, "wonb", [128, 256])
        snk, snkr = self.sb(P, "snk", [128, 32])

        def grp_of(t0):
            return min(t0 // 512, 4)

        with contextlib.ExitStack() as ph:
            self.dma(cm[:], cmat.ap().rearrange("c p n -> p c n"), [dr["cmat"]], [cmr])
            self.dma(csm[:], csmall.ap(), [dr["csmall"]], [csmr])
            self.dma(sel[:], csel.ap(), [dr["csel"]], [selr])
            self.dma(dtb[:], AP(dt_bias, 0, [[0, 128], [1, 16]]), [dr["dt_bias"]], [dtbr])
            self.dma(nA[:], AP(a_log, 0, [[0, 128], [1, 16]]), [dr["a_log"]], [nAr])
            self.dma(wonb[:], AP(w_onorm, 0, [[0, 128], [1, 256]]), [dr["w_onorm"]], [wonbr])
            self.dma(snk[:], AP(sinks, 0, [[0, 128], [1, 32]]), [dr["sinks"]], [snkr])
            self.act(nA[:], nA[:], AF.Exp, [nAr], [nAr])
            self.ts("dve", nA[:], nA[:], -1.0, None, ALU.mult, None, [nAr], [nAr])
            for v in range(2):
                pb, pr = self.bank()
                self.tr(pb[:, 0:128], NEGLm[v], 128, [cmr], [pr])
                self.cp("dve", NEGUm[v], pb[:, 0:128], [pr], [negur])
            if self.stop == "s1":
                S.flush()
                return nc
            rb, rbr = self.sb(ph, "rb", [32, 16])
            oh, ohr = self.sb(ph, "oh", [32, 384])
            ngm, ngmr = self.sb(ph, "ngm", [16, 384])
            fv, fvr = self.sb(ph, "fv", [16, 384])
            self.dma(rb[:], rel_bias.ap(), [dr["rel_bias"]], [rbr])
            self.dma(oh[:], coh.ap(), [dr["coh"]], [ohr])
            self.dma(ngm[:], cneg.ap(), [dr["cneg"]], [ngmr])
            pb, pr = self.bank()
            self.mm(pb[0:16, 0:384], rb[:], oh[:], True, True, [rbr, ohr], [pr])
            self.tt("dve", fv[:], pb[0:16, 0:384], ngm[:], ALU.add, [pr, ngmr], [fvr])
            self.dma(Z.ap(), AP(fv, 0, [[384, 16], [0, 128], [1, 384]]), [fvr], [dr["Z"]])
            bs, bsr = self.sb(ph, "bs", [128, 16, 256])
            self.ms("pool", bs[:], NEG, [bsr])
            for b in range(16):
                self.dma(bs[8 * b:8 * b + 8, :, 0:128], AP(Z, 127, [[383, 8], [128 * 384, 16], [1, 128]]),
                         [dr["Z"]], [bsr])
                self.dma(bs[8 * b:8 * b + 8, :, 128 + 8 * b:136 + 8 * b],
                         AP(Z, 255, [[383, 8], [128 * 384, 16], [1, 8]]), [dr["Z"]], [bsr])
            self.dma(ZS.ap(), bs[:], [bsr], [dr["ZS"]])
            if self.stop == "s2":
                S.flush()
                return nc
            self.rows_T(ph, ln_all.ap(), dr["ln_all"], 8, D, lnp, lnpr, "r_ln")
            self.rows_T(ph, w_conv.ap(), dr["w_conv"], 8, 3072, wcv, wcvr, "r_wc")
            bad, badr = self.sb(ph, "bad", [128, 48, 2])
            self.rows_T(ph, b_ada.ap(), dr["b_ada"], 2, 6144, bad, badr, "r_ba")
            if self.stop == "s3":
                S.flush()
                return nc
            cT32, cT32r = self.sb(ph, "cT32", [128, 8, 17])
            crow, crowr = self.sb(ph, "crow", [17, D])
            self.dma(crow[:], c_all.ap(), [dr["c_all"]], [crowr])
            self.act(crow[:], crow[:], AF.Silu, [crowr], [crowr])
            pb, pr = self.bank()
            for k in range(8):
                self.tr(pb[:, k * 17:(k + 1) * 17], crow[:, k * 128:(k + 1) * 128], 17, [crowr], [pr])
            csT, csTr = self.sb(ph, "csT", [128, 8, 17], BF16)
            self.cp("dve", csT[:], pb[:, 0:136].rearrange("p (a b) -> p a b", a=8), [pr], [csTr])
            wa = [self.sb(ph, "wa%d" % i, [128, 8, 512], BF16) for i in range(2)]
            n = 0
            for l in range(2):
                for g in range(12):
                    wt, wr_ = wa[n % 2]
                    n += 1
                    self.dma(wt[:], w_ada.ap()[l].rearrange("(k p) n -> p k n", p=128)[:, :, g * 512:(g + 1) * 512],
                             [dr["w_ada"]], [wr_], q="pool")
                    pb, pr = self.bank()
                    for mmi in range(4):
                        for k in range(8):
                            self.mm(pb[:, mmi * 17:(mmi + 1) * 17], wt[:, k, mmi * 128:(mmi + 1) * 128], csT[:, k, :],
                                    k == 0, k == 7, [wr_, csTr], [pr])
                    self.tt("dve", modT[:, l, g * 4:(g + 1) * 4, :],
                            pb[:, 0:68].rearrange("p (a b) -> p a b", a=4),
                            AP(bad, (g * 4) * 2 + l, [[96, 128], [2, 4], [0, 17]]), ALU.add, [pr, badr], [modr])
                for c0 in (8, 32):
                    self.ts("dve", modT[:, l, c0:c0 + 8, :], modT[:, l, c0:c0 + 8, :], 1.0, None, ALU.add, None,
                            [modr], [modr])
            if self.stop == "s4":
                S.flush()
                return nc
            xin = [self.sb(ph, "xin%d" % i, [128, D]) for i in range(2)]
            xst = [self.sb(ph, "xst%d" % i, [128, 8, 128]) for i in range(2)]
            for t in range(NT):
                xi, xir = xin[t % 2]
                xo, xor_ = xst[t % 2]
                self.dma(xi[:], x_tok.ap()[t * 128:(t + 1) * 128, :], [dr["x_tok"]], [xir])
                for half in range(2):
                    pb, pr = self.bank()
                    for kk in range(4):
                        k = half * 4 + kk
                        self.tr(pb[:, kk * 128:(kk + 1) * 128], xi[:, k * 128:(k + 1) * 128], 128, [xir], [pr])
                    self.cp("act", xo[:, half * 4:half * 4 + 4, :], pb[:].rearrange("p (a b) -> p a b", a=4), [pr], [xor_])
                if self.stop != "s6":
                    self.dma(xs.ap()[:, :, t * 128:(t + 1) * 128], xo[:], [xor_], [dr["xs"]])
                if self.stop not in ("s5", "s6") and not (self.stop == "s7" and t == 16):
                    self.modulate(0, 0, xo, xor_, t * 128, 128, hT, hTg[grp_of(t * 128)], modT, modr, ph if t == 0 else None)
            S.flush()
            if self.stop in ("s5", "s6", "s7", "s8", "s9"):
                return nc

        self.__dict__.update({k_: v_ for k_, v_ in locals().items() if k_ not in ("self", "ph", "P")})
        stages = []
        for l in range(2):
            stages += [("gdn%d" % l, self.gdn_phase, l), ("swa%d" % l, self.swa_phase, l), ("mlp%d" % l, self.mlp_phase, l)]
        if self.stop != "setup":
            for nm, fn, l in stages:
                fn(l)
                if self.stop == nm or (self.stop or "").startswith("w") and nm == "swa0":
                    break
        S.flush()
        self.st.close()
        return nc

    def mlp_pieces(self, l):
        w_in_l = self.w_in.ap()[l].rearrange("(k p) n -> p k n", p=128)
        wsrc = lambda w: w.ap()[l].rearrange("(k p) n -> p k n", p=128)
        GA0, GB0 = 5648, 6672
        out = []
        for mh in range(2):
            cs = slice(mh * 512, (mh + 1) * 512)
            out.append((("pa", mh), wsrc(self.w_pa)[:, :, cs], "w_pa"))
            out.append((("ga", mh), w_in_l[:, :, GA0 + mh * 512:GA0 + (mh + 1) * 512], "w_in"))
            out.append((("pb", mh), wsrc(self.w_pb)[:, :, cs], "w_pb"))
            out.append((("gb", mh), w_in_l[:, :, GB0 + mh * 512:GB0 + (mh + 1) * 512], "w_in"))
        for mh in range(2):
            out.append((("wo", mh), wsrc(self.w_out)[:, :, mh * 512:(mh + 1) * 512], "w_out"))
        for fg in range(8):
            out.append((("wu", fg), wsrc(self.w_up)[:, :, fg * 512:(fg + 1) * 512], "w_up"))
        wdv = self.w_down.ap()[l].rearrange("(fc p) n -> p fc n", p=128)
        for mh in range(2):
            for fg in range(4):
                out.append((("wd", mh, fg), wdv[:, fg * 8:(fg + 1) * 8, mh * 512:(mh + 1) * 512], "w_down"))
        return out

    def modulate(self, l, which, src, srcr, t0, tw, dst, dstr, modT, modr, alloc_stack):
        sh0 = 0 if which == 0 else 24
        sc0 = 8 if which == 0 else 32
        if alloc_stack is not None:
            self.modtmp, self.modtmpr = self.sb(alloc_stack, "modtmp", [128, 128])
        if t0 < TP:
            for k in range(8):
                self.ts("dve", dst[:, k, t0:t0 + tw], src[:, k, 0:tw], modT[:, l, sc0 + k, 0:1], modT[:, l, sh0 + k, 0:1],
                        ALU.mult, ALU.add, [srcr, modr], [dstr])
        else:
            tmp, tmpr = self.modtmp, self.modtmpr
            for k in range(8):
                scb = AP(modT, l * 48 * 17 + (sc0 + k) * 17 + 1, [[2 * 48 * 17, 128], [1, 16], [0, 8]])
                shb = AP(modT, l * 48 * 17 + (sh0 + k) * 17 + 1, [[2 * 48 * 17, 128], [1, 16], [0, 8]])
                self.tt("dve", tmp[:].rearrange("p (b i) -> p b i", b=16), src[:, k, 0:128].rearrange("p (b i) -> p b i", b=16),
                        scb, ALU.mult, [srcr, modr], [tmpr])
                if self.stop == "s8":
                    continue
                self.tt("dve", dst[:, k, t0:t0 + 128].rearrange("p (b i) -> p b i", b=16),
                        tmp[:].rearrange("p (b i) -> p b i", b=16), shb, ALU.add, [tmpr, modr], [dstr])

    def gdn_phase(self, l):
        S, dr = self.S, self.dr
        hT, hTg, cm = self.hT, self.hTg, self.cm
        cmr = self.cmr
        ident, ones = self.ident, self.ones
        w_in_l = self.w_in.ap()[l].rearrange("(k p) n -> p k n", p=128)
        hrd = list(hTg)
        with contextlib.ExitStack() as ph:
            sb = lambda name, shape, dt=F32: self.sb(ph, name, shape, dt)
            wba, wbar = sb("wba", [128, 8, 16], BF16)
            self.dma(wba[:], w_in_l[:, :, 4096:4112], [dr["w_in"]], [wbar], q="pool")
            beta, betar = sb("beta", [128, NT, 8])
            gtok, gtokr = sb("gtok", [128, NT, 8])
            eg, egr = sb("eg", [128, NT, 8])
            egrev, egrevr = sb("egrev", [128, NT, 8])
            negegb, negegbr = sb("negegb", [128, NT, 8])
            glb, glbr = sb("glb", [128, 128])
            glS, glSr = sb("glS", [128, 128])
            tmpa, tmpar = sb("tmpa", [128, NT, 8])
            pb, pr = self.bank()
            for t in range(NT):
                for k in range(8):
                    self.mm(pb[:, t * 16:(t + 1) * 16], hT[:, k, t * 128:(t + 1) * 128], wba[:, k, :], k == 0, k == 7,
                            hrd + [wbar], [pr])
            pv = pb[:, 0:NT * 16].rearrange("p (t c) -> p t c", t=NT)
            self.act(beta[:], pv[:, :, 0:8], AF.Sigmoid, [pr], [betar])
            self.tt("dve", tmpa[:], pv[:, :, 8:16], AP(self.dtb, l * 8, [[16, 128], [0, NT], [1, 8]]), ALU.add,
                    [pr, self.dtbr], [tmpar])
            self.act(tmpa[:], tmpa[:], AF.Exp, [tmpar], [tmpar])
            self.act(tmpa[:], tmpa[:], AF.Ln, [tmpar], [tmpar], bias=1.0)
            self.tt("dve", gtok[:], tmpa[:], AP(self.nA, l * 8, [[16, 128], [0, NT], [1, 8]]), ALU.mult,
                    [tmpar, self.nAr], [gtokr])
            pb1, pr1 = self.bank()
            pb2, pr2 = self.bank()
            for t in range(NT):
                v = 1 if t == 16 else 0
                self.mm(pb1[:, t * 8:(t + 1) * 8], self.Umat[v], gtok[:, t, :], True, True, [cmr, gtokr], [pr1])
                self.mm(pb2[:, t * 8:(t + 1) * 8], self.Bmat[v], gtok[:, t, :], True, True, [cmr, gtokr], [pr2])
            self.act(eg[:], pb1[:, 0:NT * 8].rearrange("p (t c) -> p t c", t=NT), AF.Exp, [pr1], [egr])
            self.act(egrev[:], pb2[:, 0:NT * 8].rearrange("p (t c) -> p t c", t=NT), AF.Exp, [pr2], [egrevr])
            self.stt("dve", negegb[:], eg[:], -1.0, beta[:], ALU.mult, ALU.mult, [egr, betar], [negegbr])
            negbeta, negbetar = sb("negbeta", [128, NT, 8])
            self.ts("dve", negbeta[:], beta[:], -1.0, None, ALU.mult, None, [betar], [negbetar])
            pb, pr = self.bank()
            self.mm(pb[:, 0:128], ones, gtok[:, 0:16, :], True, True, [cmr, gtokr], [pr])
            self.act(glb[:], pb[:, 0:128], AF.Exp, [pr], [glbr])
            gm, gmr = sb("gm", [128, 8, 16])
            self.tt("dve", gm[:], AP(gtok, 16 * 8, [[NT * 8, 128], [1, 8], [0, 16]]),
                    AP(self.csm, 0, [[64, 128], [0, 8], [1, 16]]), ALU.mult, [gtokr, self.csmr], [gmr])
            pb, pr = self.bank()
            self.mm(pb[:, 0:128], ones, gm[:].rearrange("p a b -> p (a b)"), True, True, [cmr, gmr], [pr])
            self.act(glS[:], pb[:, 0:128], AF.Exp, [pr], [glSr])
            scr, scrr = sb("scr", [48, 3, 128])

            wg, wgr = sb("wg", [128, 8, 4, 128], BF16)
            UW = 3 + TP + 16 * 11
            ub = [sb("u%d" % i, [128, UW]) for i in range(2)]
            cb = [sb("c%d" % i, [128, T]) for i in range(3)]
            oaT, oaTr = sb("oaT", [128, T], BF16)
            Sst, Sstr = sb("Sst", [128, 128])
            Sb, Sbr = sb("Sb", [128, 16, 128])
            kTm, kTmr = sb("kTm", [128, 16, 128])
            qTm, qTmr = sb("qTm", [128, 16, 128])
            kdm, kdmr = kTm, kTmr
            hs, hsr = sb("hs", [128, 8, NT])
            cst, cstr = sb("cst", [128, 2, 3, 128])
            cso, csor = sb("cso", [48, 384])
            NS = 4
            slots = []
            for i in range(NS):
                d_ = {}
                for nm in ["A1", "Ds", "DTs", "DTi", "X0", "Y0", "P0", "X1", "Y1", "P1", "XF", "YF", "T0", "T1", "W", "bv", "kegb"]:
                    d_[nm] = sb("m%d_%s" % (i, nm), [128, 128])
                d_["Xo"], d_["Yo"], d_["Gs"], d_["Fs"] = d_["A1"], d_["Ds"], d_["DTs"], d_["DTi"]
                slots.append(d_)
            outs = []
            for par in range(2):
                row = []
                for i in range(NS):
                    row.append({nm: sb("o%d%d_%s" % (par, i, nm), [128, 128]) for nm in ["U0", "WmT", "QK", "kd", "zw"]})
                outs.append(row)
            mats = {nm: sb("m_" + nm, [128, 128]) for nm in ["Xp", "vn", "t1", "o", "of", "junk"]}
            sm, smr = sb("sm", [128, 4])
            if self.debug:
                self.free_probe("gdn")
            for i in range(2):
                self.ms("pool", ub[i][0][:, 0:3], 0.0, [ub[i][1]])
            self.ms("pool", qTm[:], 0.0, [qTmr])

            for h in range(8):
                cols = [h * 128, 1024 + h * 128, 2048 + h * 128, 3072 + h * 128]
                for s in range(4):
                    self.dma(wg[:, :, s, :], w_in_l[:, :, cols[s]:cols[s] + 128], [dr["w_in"]], [wgr], q="pool")
                self.dma(Sb[:], self.sd_in.ap()[l, :, h].rearrange("b k v -> k b v"), [dr["sd_in"]], [Sbr])
                pcs = self.mlp_pieces(l)
                for pi in range(h * 4, min(h * 4 + 4, 26)):
                    self.dma(self.wsc.ap()[l, pi], pcs[pi][1], [dr[pcs[pi][2]]], [self.wscr[l][pi]], q="pool")
                self.dma(scr[:], AP(self.sc_in, l * 48 * 3072 + h * 128, [[3072, 48], [1024, 3], [1, 128]]),
                         [dr["sc_in"]], [scrr])
                self.ms("pool", Sst[:], 0.0, [Sstr])
                for s in range(3):
                    u, ur = ub[s % 2]
                    self.ms("pool", u[:, 0:3], 0.0, [ur])
                    for (g0, gw) in GROUPS:
                        pb, pr = self.bank()
                        for k in range(8):
                            self.mm(pb[:, 0:gw], wg[:, k, s, :], hT[:, k, g0:g0 + gw], k == 0, k == 7, hrd + [wgr], [pr])
                        if g0 < TP:
                            self.cp("act", u[:, 3 + g0:3 + g0 + gw], pb[:, 0:gw], [pr], [ur])
                        else:
                            self.cp("act", AP(u, 3 + TP + 3, [[UW, 128], [11, 16], [1, 8]]),
                                    pb[:, 0:128].rearrange("p (b i) -> p b i", b=16), [pr], [ur])
                    pb, pr = self.bank()
                    ch = s * 8 + h
                    self.tr(pb[:, 0:48], scr[:, s, :], 48, [scrr], [pr])
                    self.cp("dve", AP(u, 3 + TP, [[UW, 128], [11, 16], [1, 3]]),
                            pb[:, 0:48].rearrange("p (b j) -> p b j", b=16), [pr], [ur])
                    c, cr = cb[s]
                    for j in range(4):
                        wj = self.wcv[:, ch, l * 4 + j:l * 4 + j + 1]
                        srcp = u[:, j:j + TP]
                        srcs = AP(u, 3 + TP + j, [[UW, 128], [11, 16], [1, 8]])
                        dsts = c[:, TP:T].rearrange("p (b i) -> p b i", b=16)
                        if j == 0:
                            self.ts("dve", c[:, 0:TP], srcp, wj, None, ALU.mult, None, [ur, self.wcvr], [cr])
                            self.ts("dve", dsts, srcs, wj, None, ALU.mult, None, [ur, self.wcvr], [cr])
                        else:
                            self.stt("dve", c[:, 0:TP], srcp, wj, c[:, 0:TP], ALU.mult, ALU.add, [ur, self.wcvr, cr], [cr])
                            self.stt("dve", dsts, srcs, wj, dsts, ALU.mult, ALU.add, [ur, self.wcvr, cr], [cr])
                    self.act(c[:], c[:], AF.Silu, [cr], [cr])
                for ti, t in enumerate((15, 16)):
                    pb, pr = self.bank()
                    for s in range(3):
                        for k in range(8):
                            self.mm(pb[:, s * 128:(s + 1) * 128], hT[:, k, t * 128:(t + 1) * 128], wg[:, k, s, :],
                                    k == 0, k == 7, hrd + [wgr], [pr])
                    self.cp("act", cst[:, ti, :, :], pb[:, 0:384].rearrange("p (s c) -> p s c", s=3), [pr], [cstr])
                self.dma(AP(self.p_sc, l * 3 * 3072 + h * 128, [[3072, 3], [1024, 3], [1, 128]]), cst[125:128, 0, :, :],
                         [cstr], [dr["p_sc"]])
                pb, pr = self.bank()
                self.mm(pb[0:48, 0:384], self.selg, cst[:, 1, :, :].rearrange("p s c -> p (s c)"), True, True,
                        [self.csmr, cstr], [pr])
                self.cp("dve", cso[:], pb[0:48, 0:384], [pr], [csor])
                self.dma(AP(self.s_sc, l * 48 * 3072 + h * 128, [[3072, 48], [1024, 3], [1, 128]]),
                         cso[:].rearrange("p (s c) -> p s c", s=3), [csor], [dr["s_sc"]])
                (cq, cqr), (ck, ckr), (cv, cvr) = cb
                pbn, prn = self.bank()
                for qi, (c, cr) in enumerate(((cq, cqr), (ck, ckr))):
                    u, ur = ub[qi]
                    self.act(u[:, 0:T], c[:], AF.Square, [cr], [ur])
                    for t in range(NT):
                        self.mm(pbn[:, qi * NT + t:qi * NT + t + 1], u[:, t * 128:(t + 1) * 128], ones[:, 0:1], True, True,
                                [ur, cmr], [prn])
                self.act(hs[:, 0:2, :], pbn[:, 0:2 * NT].rearrange("p (a t) -> p a t", a=2), AF.Sqrt, [prn], [hsr], bias=1e-6)
                self.S.op("dve", lambda e: e.reciprocal(out=hs[:, 0:2, :], in_=hs[:, 0:2, :]), rd=[hsr], wr=[hsr])
                self.tt("dve", hs[:, 2, :], hs[:, 1, :], hs[:, 1, :], ALU.mult, [hsr], [hsr])
                self.ts("dve", hs[:, 0, :], hs[:, 0, :], 128 ** -0.5, None, ALU.mult, None, [hsr], [hsr])
                col = lambda tns: AP(tns, h, [[NT * 8, 128], [8, NT]])
                self.tt("dve", hs[:, 3, :], col(negbeta), hs[:, 2, :], ALU.mult, [negbetar, hsr], [hsr])
                self.tt("dve", hs[:, 4, :], col(beta), hs[:, 1, :], ALU.mult, [betar, hsr], [hsr])
                self.tt("dve", hs[:, 5, :], col(negegb), hs[:, 2, :], ALU.mult, [negegbr, hsr], [hsr])
                self.tt("dve", hs[:, 6, :], col(eg), hs[:, 0, :], ALU.mult, [egr, hsr], [hsr])
                self.ms("pool", kTm[:], 0.0, [kTmr])
                Lc = locals()
                groups = [list(range(g0_, min(g0_ + NS, NT))) for g0_ in range(0, NT, NS)]

                def lockstep(gens):
                    gens = list(gens)
                    while gens:
                        for g_ in list(gens):
                            try:
                                next(g_)
                            except StopIteration:
                                gens.remove(g_)
                prev = None
                for gi, grp in enumerate(groups):
                    gens = [self.gdn_solve(l, h, t, slots[i], outs[gi % 2][i], Lc) for i, t in enumerate(grp)]
                    if prev is not None:
                        gens.append(self.gdn_state(l, h, prev[0], outs[prev[1] % 2], Lc))
                    lockstep(gens)
                    prev = (grp, gi)
                lockstep([self.gdn_state(l, h, prev[0], outs[prev[1] % 2], Lc)])
                self.dma(self.p_sd.ap()[l, h], Sst[:], [Sstr], [dr["p_sd"]])
                self.dma(self.s_sd.ap()[l, :, h].rearrange("b k v -> k b v"), Sb[:], [Sbr], [dr["s_sd"]])
                self.dma(self.oa.ap()[h], oaT[:], [oaTr], [dr["oa"]])
            S.flush()

    def gdn_solve(self, l, h, t, m, O, L):
        cmr = self.cmr
        ident = self.ident
        (cq, cqr), (ck, ckr), (cv, cvr) = L["cb"]
        tc = slice(t * 128, (t + 1) * 128)
        v = 1 if t == 16 else 0
        Um, Bm_, NL, NU = self.Umat[v], self.Bmat[v], self.NEGLm[v], self.NEGUm[v]
        A1, A1r = m["A1"]
        Ds, Dsr = m["Ds"]
        DTs, DTsr = m["DTs"]
        DTi, DTir = m["DTi"]
        QK, QKr = O["QK"]
        kd, kdr = O["kd"]
        bv, bvr = m["bv"]
        kegb, kegbr = m["kegb"]
        zw, zwr = O["zw"]
        hT, hrd, wg, wgr = self.hT, L["hrd"], L["wg"], L["wgr"]
        pb, pr = self.bank()
        self.tr(pb[:, 0:128], ck[:, tc], 128, [ckr], [pr])
        self.tr(pb[:, 128:256], cv[:, tc], 128, [cvr], [pr])
        self.ts("dve", kd[:], pb[:, 0:128], L["egrev"][:, t, h:h + 1], None, ALU.mult, None, [pr, L["egrevr"]], [kdr])
        hs, hsr = L["hs"], L["hsr"]
        self.ts("dve", bv[:], pb[:, 128:256], hs[:, 4, t:t + 1], None, ALU.mult, None, [pr, hsr], [bvr])
        self.ts("dve", kegb[:], pb[:, 0:128], hs[:, 5, t:t + 1], None, ALU.mult, None, [pr, hsr], [kegbr])
        yield
        pb, pr = self.bank()
        for k in range(8):
            self.mm(pb[:, 0:128], hT[:, k, tc], wg[:, k, 3, :], k == 0, k == 7, hrd + [wgr], [pr])
        self.act(zw[:], pb[:, 0:128], AF.Silu, [pr], [zwr])
        self.ts("dve", A1[:], Um, L["gtok"][:, t, h:h + 1], None, ALU.mult, None, [cmr, L["gtokr"]], [A1r])
        yield
        self.tt("pool", zw[:], zw[:], self.wonb[:, l * 128:(l + 1) * 128], ALU.mult, [zwr, self.wonbr], [zwr])
        pb, pr = self.bank()
        self.mm(pb[:, 0:128], A1[:], Bm_, True, False, [A1r, cmr], [pr])
        self.mm(pb[:, 0:128], ident, NL, False, True, [cmr], [pr])
        self.act(Ds[:], pb[:, 0:128], AF.Exp, [pr], [Dsr])
        yield
        X, Xr = m["X0"]
        Y, Yr = m["Y0"]
        Pm, Pr = m["P0"]
        pb, pr = self.bank()
        self.mm(pb[:, 0:128], ck[:, tc], ck[:, tc], True, True, [ckr], [pr])
        self.stt("dve", X[:], pb[:, 0:128], hs[:, 3, t:t + 1], Ds[:], ALU.mult, ALU.mult,
                 [pr, Dsr, hsr], [Xr])
        self.tt("pool", DTi[:], Ds[:], ident, ALU.add, [Dsr, cmr], [DTir])
        yield
        pb, pr = self.bank()
        self.tr(pb[:, 0:128], X[:], 128, [Xr], [pr])
        self.cp("act", Y[:], pb[:, 0:128], [pr], [Yr])
        pb, pr = self.bank()
        self.mm(pb[:, 0:128], cq[:, tc], ck[:, tc], True, True, [ckr, cqr], [pr])
        self.stt("dve", DTs[:], pb[:, 0:128], hs[:, 0, t:t + 1], DTi[:], ALU.mult, ALU.mult, [pr, DTir, hsr], [DTsr])
        yield
        pb, pr = self.bank()
        self.tr(pb[:, 0:128], DTs[:], 128, [DTsr], [pr])
        self.cp("act", QK[:], pb[:, 0:128], [pr], [QKr])
        if t == 16:
            self.tt("pool", Pm[:], Y[:], ident, ALU.add, [Yr, cmr], [Pr])
            yield
            cur = 0
            for it in range(2):
                last = it == 1
                nx = 1 - cur
                X2, X2r = m["X%d" % nx]
                Y2, Y2r = m["Y%d" % nx]
                P2, P2r = m["P%d" % nx] if not last else m["W"]
                pb, pr = self.bank()
                self.mm(pb[:, 0:128], Y[:], X[:], True, True, [Yr, Xr], [pr])
                self.cp("act", X2[:], pb[:, 0:128], [pr], [X2r])
                if not last:
                    pb, pr = self.bank()
                    self.mm(pb[:, 0:128], X[:], Y[:], True, True, [Yr, Xr], [pr])
                    self.cp("dve", Y2[:], pb[:, 0:128], [pr], [Y2r])
                yield
                pb, pr = self.bank()
                self.mm(pb[:, 0:128], X2[:], Pm[:], True, True, [X2r, Pr], [pr])
                self.tt("dve", P2[:], pb[:, 0:128], Pm[:], ALU.add, [pr, Pr], [P2r])
                yield
                X, Xr, Y, Yr, Pm, Pr = X2, X2r, Y2, Y2r, P2, P2r
                cur = nx
            yield from self.gdn_solve_tail(m, O)
            return
        XF, XFr = m["XF"]
        YF, YFr = m["YF"]
        self.cp("pool", XF[:], X[:], [Xr], [XFr])
        self.cp("pool", YF[:], Y[:], [Yr], [YFr])
        Tm, Tr = m["T0"]
        self.tt("dve", X[:], X[:], self.bd16, ALU.mult, [Xr, cmr], [Xr])
        self.tt("dve", Y[:], Y[:], self.bd16, ALU.mult, [Yr, cmr], [Yr])
        yield
        self.tt("pool", Pm[:], Y[:], ident, ALU.add, [Yr, cmr], [Pr])
        self.tt("pool", Tm[:], X[:], ident, ALU.add, [Xr, cmr], [Tr])
        cur = 0
        for it in range(3):
            nx = 1 - cur
            X2, X2r = m["X%d" % nx]
            Y2, Y2r = m["Y%d" % nx]
            P2, P2r = m["P%d" % nx]
            T2, T2r = m["T%d" % nx]
            pb, pr = self.bank()
            self.mm(pb[:, 0:128], Y[:], X[:], True, True, [Yr, Xr], [pr])
            self.cp("act", X2[:], pb[:, 0:128], [pr], [X2r])
            pb, pr = self.bank()
            self.mm(pb[:, 0:128], X[:], Y[:], True, True, [Yr, Xr], [pr])
            self.cp("act", Y2[:], pb[:, 0:128], [pr], [Y2r])
            yield
            pb, pr = self.bank()
            self.mm(pb[:, 0:128], X2[:], Pm[:], True, True, [X2r, Pr], [pr])
            self.tt("dve", P2[:], pb[:, 0:128], Pm[:], ALU.add, [pr, Pr], [P2r])
            pb, pr = self.bank()
            self.mm(pb[:, 0:128], Y2[:], Tm[:], True, True, [Y2r, Tr], [pr])
            self.tt("dve", T2[:], pb[:, 0:128], Tm[:], ALU.add, [pr, Tr], [T2r])
            yield
            X, Xr, Y, Yr, Pm, Pr, Tm, Tr = X2, X2r, Y2, Y2r, P2, P2r, T2, T2r
            cur = nx
        Xo, Xor = m["Xo"]
        Yo, Yor = m["Yo"]
        Gs, Gsr = m["Gs"]
        Fs, Fsr = m["Fs"]
        for bsz in (16, 32, 64):
            lastl = bsz == 64
            nx = 1 - cur
            P2, P2r = m["P%d" % nx] if not lastl else m["W"]
            T2, T2r = m["T%d" % nx]
            self.tt("pool", Xo[:], XF[:], self.mo[bsz], ALU.mult, [XFr, cmr], [Xor])
            if not lastl:
                self.tt("pool", Yo[:], YF[:], self.moT[bsz], ALU.mult, [YFr, cmr], [Yor])
            yield
            pb, pr = self.bank()
            self.mm(pb[:, 0:128], Xo[:], Pm[:], True, True, [Xor, Pr], [pr])
            self.cp("act", Gs[:], pb[:, 0:128], [pr], [Gsr])
            if not lastl:
                pb, pr = self.bank()
                self.mm(pb[:, 0:128], Yo[:], Tm[:], True, True, [Yor, Tr], [pr])
                self.cp("act", Fs[:], pb[:, 0:128], [pr], [Fsr])
            yield
            pb, pr = self.bank()
            self.mm(pb[:, 0:128], Tm[:], Gs[:], True, True, [Tr, Gsr], [pr])
            self.tt("dve", P2[:], pb[:, 0:128], Pm[:], ALU.add, [pr, Pr], [P2r])
            if not lastl:
                pb, pr = self.bank()
                self.mm(pb[:, 0:128], Pm[:], Fs[:], True, True, [Pr, Fsr], [pr])
                self.tt("dve", T2[:], pb[:, 0:128], Tm[:], ALU.add, [pr, Tr], [T2r])
            yield
            Pm, Pr, Tm, Tr = P2, P2r, T2, T2r
            cur = nx
        yield from self.gdn_solve_tail(m, O)

    def gdn_solve_tail(self, m, O):
        W, Wr = m["W"]
        bv, bvr = m["bv"]
        kegb, kegbr = m["kegb"]
        U0, U0r = O["U0"]
        WmT, WmTr = O["WmT"]
        pb, pr = self.bank()
        self.mm(pb[:, 0:128], W[:], bv[:], True, True, [Wr, bvr], [pr])
        self.cp("act", U0[:], pb[:, 0:128], [pr], [U0r])
        pb, pr = self.bank()
        self.mm(pb[:, 0:128], kegb[:], W[:], True, True, [Wr, kegbr], [pr])
        self.cp("act", WmT[:], pb[:, 0:128], [pr], [WmTr])
        yield

    def gdn_state(self, l, h, tiles, Os, L):
        cmr = self.cmr
        m = L["mats"]
        (cq, cqr), (ck, ckr), (cv, cvr) = L["cb"]
        Sst, Sstr, Sb, Sbr = L["Sst"], L["Sstr"], L["Sb"], L["Sbr"]
        kTm, kTmr, qTm, qTmr, kdm, kdmr = L["kTm"], L["kTmr"], L["qTm"], L["qTmr"], L["kdm"], L["kdmr"]
        Xp, Xpr = m["Xp"]
        vn, vnr = m["vn"]
        t1, t1r = m["t1"]
        o, orr = m["o"]
        of, ofr = m["of"]
        junk, junkr = m["junk"]
        sm, smr = L["sm"], L["smr"]
        for i, t in enumerate(tiles):
            O = Os[i]
            U0, U0r = O["U0"]
            WmT, WmTr = O["WmT"]
            QK, QKr = O["QK"]
            kd, kdr = O["kd"]
            zw, zwr = O["zw"]
            tc = slice(t * 128, (t + 1) * 128)
            pkS, pkSr = self.bank()
            if t < 16:
                self.mm(pkS[:, 0:128], WmT[:], Sst[:], True, True, [WmTr, Sstr], [pkSr])
            else:
                dg = lambda tns: AP(tns, 0, [[2048, 128], [136, 16], [1, 8]])
                self.cp("pool", dg(kTm), WmT[:].rearrange("p (b i) -> p b i", b=16), [WmTr], [kTmr])
                self.cp("pool", dg(qTm), cq[:, tc].rearrange("p (b i) -> p b i", b=16), [cqr], [qTmr])
                for b in range(16):
                    self.mm(pkS[:, 0:128], kTm[:, b, :], Sb[:, b, :], b == 0, b == 15, [kTmr, Sbr], [pkSr])
            self.tt("dve", vn[:], pkS[:, 0:128], U0[:], ALU.add, [pkSr, U0r], [vnr])
            pqS, pqSr = self.bank()
            if t < 16:
                self.mm(pqS[:, 0:128], cq[:, tc], Sst[:], True, True, [cqr, Sstr], [pqSr])
            else:
                for b in range(16):
                    self.mm(pqS[:, 0:128], qTm[:, b, :], Sb[:, b, :], b == 0, b == 15, [qTmr, Sbr], [pqSr])
            self.act(t1[:], pqS[:, 0:128], AF.Identity, [pqSr, L["hsr"]], [t1r], scale=L["hs"][:, 6, t:t + 1])
            yield
            if t < 16:
                pb, pr = self.bank()
                self.mm(pb[:, 0:128], kd[:], vn[:], True, True, [kdr, vnr], [pr])
                self.stt("dve", Sst[:], Sst[:], L["glb"][:, t * 8 + h:t * 8 + h + 1], pb[:, 0:128], ALU.mult, ALU.add,
                         [Sstr, L["glbr"], pr], [Sstr])
            else:
                self.tt("dve", kdm[:], AP(kd, 0, [[128, 128], [0, 16], [1, 128]]),
                        AP(self.csm, 0, [[64, 128], [1, 16], [0, 128]]), ALU.mult, [kdr, self.csmr], [kdmr])
                for b4 in range(4):
                    pb, pr = self.bank()
                    for bb in range(4):
                        b = b4 * 4 + bb
                        self.mm(pb[:, bb * 128:(bb + 1) * 128], kdm[:, b, :], vn[:], True, True, [kdmr, vnr], [pr])
                    for bb in range(4):
                        b = b4 * 4 + bb
                        self.stt("dve", Sb[:, b, :], Sb[:, b, :], L["glS"][:, h * 16 + b:h * 16 + b + 1],
                                 pb[:, bb * 128:(bb + 1) * 128], ALU.mult, ALU.add, [Sbr, L["glSr"], pr], [Sbr])
            pb, pr = self.bank()
            self.mm(pb[:, 0:128], QK[:], vn[:], True, True, [QKr, vnr], [pr])
            self.tt("dve", o[:], pb[:, 0:128], t1[:], ALU.add, [pr, t1r], [orr])
            yield
            self.act(junk[:], o[:], AF.Square, [orr], [junkr, smr], accum=sm[:, 0:1])
            yield
            self.act(sm[:, 1:2], sm[:, 0:1], AF.Sqrt, [smr], [smr], bias=1e-6, scale=1.0 / 128)
            yield
            self.S.op("dve", lambda e: e.reciprocal(out=sm[:, 2:3], in_=sm[:, 1:2]), rd=[smr], wr=[smr])
            yield
            self.stt("dve", of[:], o[:], sm[:, 2:3], zw[:], ALU.mult, ALU.mult, [orr, smr, zwr], [ofr])
            yield
            pb, pr = self.bank()
            self.tr(pb[:, 0:128], of[:], 128, [ofr], [pr])
            self.cp("act", L["oaT"][:, tc], pb[:, 0:128], [pr], [L["oaTr"]])
            yield

    def swa_phase(self, l):
        S, dr = self.S, self.dr
        hT, hrd, cmr = self.hT, list(self.hTg), self.cmr
        w_in_l = self.w_in.ap()[l].rearrange("(k p) n -> p k n", p=128)
        QB0, KB0, VB0 = 4112, 5136, 5392
        with contextlib.ExitStack() as ph:
            sb = lambda name, shape, dt=F32: self.sb(ph, name, shape, dt)
            wk, wkr = sb("wk", [128, 8, 4, 2, 64], BF16)
            wv, wvr = sb("wv", [128, 8, 4, 2, 64], BF16)
            wstg, wstgr = sb("wstg", [128, 8, 256], BF16)
            for (wdst, wdr, c0) in ((wk, wkr, KB0), (wv, wvr, VB0)):
                self.dma(wstg[:], w_in_l[:, :, c0:c0 + 256], [dr["w_in"]], [wstgr], q="pool")
                for k in range(8):
                    srcv = wstg[:, k, :].rearrange("p (a d) -> p a d", a=4)
                    self.cp("act" if k % 2 else "pool", wdst[:, k, :, 0, :], srcv, [wstgr], [wdr])
                    self.cp("pool" if k % 2 else "act", wdst[:, k, :, 1, :], srcv, [wstgr], [wdr])
            kcT, kcTr = sb("kcT", [128, 16, 4, 128], BF16)
            vcd, vcdr = sb("vcd", [128, 16, 4, 2, 64], BF16)
            for dup in range(2):
                for kp in range(4):
                    self.dma(vcd[:, :, kp, dup, :], self.cv_in.ap()[l][:, :, kp * 64:(kp + 1) * 64].rearrange("b p d -> p b d"),
                             [dr["cv_in"]], [vcdr], q="pool")
            kst, kstr = sb("kst", [128, 4, 4, 2, 64])
            def kgroup_load(b4):
                for dup in range(2):
                    for kp in range(4):
                        self.dma(kst[:, :, kp, dup, :],
                                 self.ck_in.ap()[l, b4 * 4:(b4 + 1) * 4][:, :, kp * 64:(kp + 1) * 64].rearrange("b p d -> p b d"),
                                 [dr["ck_in"]], [kstr])

            def kgroup_tr(b4):
                for bb in range(4):
                    pb, pr = self.bank()
                    for kp in range(4):
                        self.tr(pb[:, kp * 128:(kp + 1) * 128], kst[:, bb, kp, :, :].rearrange("p a d -> p (a d)"), 128,
                                [kstr], [pr])
                    self.cp("act", kcT[:, b4 * 4 + bb, :, :], pb[:].rearrange("p (a c) -> p a c", a=4), [pr], [kcTr])
            kgroup_load(0)
            if self.stop == "w0a":
                S.flush()
                return
            kdT, kdTr = sb("kdT", [128, 4, T], BF16)
            vd, vdr = sb("vd", [128, NT, 512], BF16)
            ktv, ktvr = sb("ktv", [128, 2, 2, 512])
            for kp in range(4):
                for (g0, gw) in GROUPS:
                    pb, pr = self.bank()
                    for k in range(8):
                        self.mm(pb[:, 0:gw], wk[:, k, kp, :, :].rearrange("p a d -> p (a d)"), hT[:, k, g0:g0 + gw],
                                k == 0, k == 7, hrd + [wkr], [pr])
                    self.cp("act", kdT[:, kp, g0:g0 + gw], pb[:, 0:gw], [pr], [kdTr])
                kgroup_tr(kp)
                if kp + 1 < 4:
                    kgroup_load(kp + 1)
            if self.stop == "w0b":
                S.flush()
                return
            for t in range(NT):
                pb, pr = self.bank()
                for k in range(8):
                    self.mm(pb[:], hT[:, k, t * 128:(t + 1) * 128], wv[:, k, :, :, :].rearrange("p a b d -> p (a b d)"),
                            k == 0, k == 7, hrd + [wvr], [pr])
                self.cp("act", vd[:, t, :], pb[:], [pr], [vdr])
                if t >= 15 and self.stop != "w0c":
                    self.cp("dve", ktv[:, 1, t - 15, :], pb[:], [pr], [ktvr])
                    pb, pr = self.bank()
                    for k in range(8):
                        self.mm(pb[:], hT[:, k, t * 128:(t + 1) * 128], wk[:, k, :, :, :].rearrange("p a b d -> p (a b d)"),
                                k == 0, k == 7, hrd + [wkr], [pr])
                    self.cp("dve", ktv[:, 0, t - 15, :], pb[:], [pr], [ktvr])
            if self.stop == "w1":
                S.flush()
                return
            def cache_outputs():
                for ci, (pout, sout, cin) in enumerate(((self.p_ck, self.s_ck, self.ck_in), (self.p_cv, self.s_cv, self.cv_in))):
                    nm_p, nm_s, nm_i = (("p_ck", "s_ck", "ck_in"), ("p_cv", "s_cv", "cv_in"))[ci]
                    src = lambda ti, p0, p1: AP(ktv, ((ci * 2 + ti) * 512), [[2048, 128], [128, 4], [1, 64]])[p0:p1]
                    self.dma(pout.ap()[l].rearrange("p (a d) -> p a d", a=4), src(0, 0, 128), [ktvr], [dr[nm_p]])
                    self.dma(sout.ap()[l, :, 0:120, :], cin.ap()[l, :, 8:128, :], [dr[nm_i]], [dr[nm_s]])
                    for b in range(16):
                        self.dma(sout.ap()[l, b, 120:128, :].rearrange("p (a d) -> p a d", a=4), src(1, 8 * b, 8 * b + 8),
                                 [ktvr], [dr[nm_s]])
            if self.stop == "w2":
                S.flush()
                return
            if self.stop == "w3":
                S.flush()
                return
            wq, wqr = sb("wq", [128, 8, 128], BF16)
            qT, qTr = sb("qT", [128, T], BF16)
            obT, obTr = sb("obT", [128, T], BF16)
            Bp, Bpr = sb("Bp", [128, 2, 256])
            Bs, Bsr = sb("Bs", [128, 2, 256])
            qTm, qTmr = sb("qTm2", [128, 16, 128], BF16)
            pTm, pTmr = sb("pTm", [128, 16, 128], BF16)
            bufsets = [(sb("scs%d" % i, [128, 2, 256]), sb("pns%d" % i, [128, 2, 256]), sb("pTs%d" % i, [128, 2, 2, 128], BF16),
                        sb("sms%d" % i, [128, 16])) for i in range(5)]
            negsnk, negsnkr = sb("negsnk", [128, 32])
            self.ts("dve", negsnk[:], self.snk[:], -1.0, None, ALU.mult, None, [self.snkr], [negsnkr])
            if self.debug:
                self.free_probe("swa")
            self.ms("pool", qTm[:], 0.0, [qTmr])
            self.ms("pool", pTm[:], 0.0, [pTmr])
            dg = lambda tns: AP(tns, 0, [[2048, 128], [136, 16], [1, 8]])
            it = 0
            for hp in range(8):
                kp = hp // 2
                self.dma(wq[:], w_in_l[:, :, QB0 + hp * 128:QB0 + (hp + 1) * 128], [dr["w_in"]], [wqr], q="pool")
                self.dma(Bp[:], AP(self.Z, (2 * hp) * 128 * 384 + 127, [[383, 128], [128 * 384, 2], [1, 256]]),
                         [dr["Z"]], [Bpr])
                self.dma(Bs[:], self.ZS.ap()[:, 2 * hp:2 * hp + 2, :], [dr["ZS"]], [Bsr])
                for (g0, gw) in GROUPS:
                    pb, pr = self.bank()
                    for k in range(8):
                        self.mm(pb[:, 0:gw], wq[:, k, :], hT[:, k, g0:g0 + gw], k == 0, k == 7, hrd + [wqr], [pr])
                    self.act(qT[:, g0:g0 + gw], pb[:, 0:gw], AF.Identity, [pr], [qTr], scale=0.125)
                self.cp("pool", dg(qTm), qT[:, TP:T].rearrange("p (b i) -> p b i", b=16), [qTr], [qTmr])
                def swa_iter(t, bufset):
                    tc = slice(t * 128, (t + 1) * 128)
                    (sc, scr_), (pn, pnr), (pT, pTr), (sm, smr) = bufset
                    k0 = 128 if t == 0 else 0
                    ks = slice(k0, 256)
                    bias, biasr = (Bs, Bsr) if t == 16 else (Bp, Bpr)
                    for a in range(2):
                        pb, pr = self.bank()
                        pa = slice(a * 64, (a + 1) * 64)
                        if t == 0:
                            self.mm(pb[:, 128:256], qT[pa, tc], kdT[pa, kp, 0:128], True, True, [qTr, kdTr], [pr])
                        elif t < 16:
                            self.mm(pb[:, 0:256], qT[pa, tc], kdT[pa, kp, (t - 1) * 128:(t + 1) * 128], True, True,
                                    [qTr, kdTr], [pr])
                        else:
                            for b in range(16):
                                self.mm(pb[:, 0:128], qTm[pa, b, :], kcT[pa, b, kp, :], b == 0, b == 15, [qTmr, kcTr], [pr])
                            self.mm(pb[:, 128:256], qT[pa, tc], kdT[pa, kp, tc], True, True, [qTr, kdTr], [pr])
                        self.tt("dve", sc[:, a, ks], pb[:, ks], bias[:, a, ks], ALU.add, [pr, biasr], [scr_])
                    yield
                    if self.stop == "w7":
                        return
                    self.S.op("dve", lambda e, o_=sm[:, 0:2], i_=sc[:, :, ks]: e.reduce_max(out=o_, in_=i_, axis=AX.X),
                              rd=[scr_], wr=[smr])
                    yield
                    if self.stop == "w8":
                        return
                    nsk = negsnk[:, l * 16 + 2 * hp:l * 16 + 2 * hp + 2]
                    self.stt("dve", sm[:, 2:4], sm[:, 0:2], -1.0, nsk, ALU.mult, ALU.min, [smr, negsnkr], [smr])
                    yield
                    for a in range(2):
                        self.act(pn[:, a, ks], sc[:, a, ks], AF.Exp, [scr_, smr], [pnr, smr], bias=sm[:, 2 + a:3 + a],
                                 accum=sm[:, 4 + a:5 + a])
                    self.tt("dve", sm[:, 6:8], sm[:, 2:4], nsk, ALU.subtract, [smr, negsnkr], [smr])
                    yield
                    self.act(sm[:, 6:8], sm[:, 6:8], AF.Exp, [smr], [smr])
                    yield
                    if self.stop == "w9":
                        return
                    self.tt("dve", sm[:, 8:10], sm[:, 4:6], sm[:, 6:8], ALU.add, [smr], [smr])
                    yield
                    self.S.op("dve", lambda e, o_=sm[:, 10:12], i_=sm[:, 8:10]: e.reciprocal(out=o_, in_=i_), rd=[smr], wr=[smr])
                    yield
                    nk = 256 - k0
                    self.tt("dve", pn[:, :, ks], pn[:, :, ks], AP(sm, 10, [[16, 128], [1, 2], [0, nk]]), ALU.mult,
                            [pnr, smr], [pnr])
                    yield
                    if self.stop == "w10":
                        return
                    pb2, pr2 = self.bank()
                    h0 = k0 // 128
                    for a in range(2):
                        for half in range(h0, 2):
                            self.tr(pb2[:, (a * 2 + half) * 128:(a * 2 + half + 1) * 128], pn[:, a, half * 128:(half + 1) * 128],
                                    128, [pnr], [pr2])
                    self.cp("act", pT[:, :, h0:2, :], pb2[:].rearrange("p (a h c) -> p a h c", a=2, h=2)[:, :, h0:2, :], [pr2], [pTr])
                    yield
                    if self.stop == "w11":
                        return
                    pb3, pr3 = self.bank()
                    vown = vd[:, t, kp * 128:(kp + 1) * 128]
                    for a in range(2):
                        oo = slice(a * 128, (a + 1) * 128)
                        if t == 0:
                            self.mm(pb3[:, oo], vown, pT[:, a, 1, :], True, True, [vdr, pTr], [pr3])
                        elif t < 16:
                            self.mm(pb3[:, oo], vd[:, t - 1, kp * 128:(kp + 1) * 128], pT[:, a, 0, :], True, False, [vdr, pTr], [pr3])
                            self.mm(pb3[:, oo], vown, pT[:, a, 1, :], False, True, [vdr, pTr], [pr3])
                        else:
                            self.cp("pool", dg(pTm), pT[:, a, 0, :].rearrange("p (b i) -> p b i", b=16), [pTr], [pTmr])
                            for b in range(16):
                                self.mm(pb3[:, oo], vcd[:, b, kp, :, :].rearrange("p a d -> p (a d)"), pTm[:, b, :],
                                        b == 0, False, [vcdr, pTmr], [pr3])
                            self.mm(pb3[:, oo], vown, pT[:, a, 1, :], False, True, [vdr, pTr], [pr3])
                    for a in range(2):
                        pa = slice(a * 64, (a + 1) * 64)
                        self.cp("act", obT[pa, tc], pb3[pa, a * 128:(a + 1) * 128], [pr3], [obTr])
                    yield

                NW = 5
                pending = list(range(NT))
                active = []
                freeb = list(range(NW))
                while pending or active:
                    while pending and freeb:
                        j = freeb.pop(0)
                        active.append((swa_iter(pending.pop(0), bufsets[j]), j))
                    for g_ in list(active):
                        try:
                            next(g_[0])
                        except StopIteration:
                            active.remove(g_)
                            freeb.append(g_[1])
                self.dma(self.ob.ap()[hp], obT[:], [obTr], [dr["ob"]])
            cache_outputs()
            S.flush()

    def layer_norm(self, r, rr, tw, gi, bi, l, L, fuse=None):
        cmr = self.cmr
        mean, meanr = L["mean"], L["meanr"]
        rstd, rstdr = L["rstd"], L["rstdr"]
        pM, pMr = self.bank()
        for k in range(8):
            self.mm(pM[:, 0:tw], self.onesm, r[:, k, 0:tw], k == 0, k == 7, [cmr, rr], [pMr])
        pQ, pQr = self.bank()
        for k in range(8):
            sq, sqr = L["sq2"][k % 2]
            self.act(sq[:, 0:tw], r[:, k, 0:tw], AF.Square, [rr], [sqr])
            self.mm(pQ[:, 0:tw], self.onesm, sq[:, 0:tw], k == 0, k == 7, [cmr, sqr], [pQr])
        self.cp("act", mean[:, 0:tw], pM[:, 0:tw], [pMr], [meanr])
        self.act(rstd[:, 0:tw], pM[:, 0:tw], AF.Square, [pMr], [rstdr])
        self.tt("dve", rstd[:, 0:tw], pQ[:, 0:tw], rstd[:, 0:tw], ALU.subtract, [pQr, rstdr], [rstdr])
        self.act(rstd[:, 0:tw], rstd[:, 0:tw], AF.Sqrt, [rstdr], [rstdr], bias=1e-5)
        self.S.op("dve", lambda e, a=rstd[:, 0:tw]: e.reciprocal(out=a, in_=a), rd=[rstdr], wr=[rstdr])
        if fuse is None:
            bc = lambda tns: AP(tns, 0, [[512, 128], [0, 8], [1, tw]])
            self.tt("dve", r[:, :, 0:tw], r[:, :, 0:tw], bc(mean), ALU.subtract, [rr, meanr], [rr])
            self.tt("dve", r[:, :, 0:tw], r[:, :, 0:tw], bc(rstd), ALU.mult, [rr, rstdr], [rr])
            for k in range(8):
                self.ts("dve", r[:, k, 0:tw], r[:, k, 0:tw], self.lnp[:, k, gi * 2 + l:gi * 2 + l + 1],
                        self.lnp[:, k, bi * 2 + l:bi * 2 + l + 1], ALU.mult, ALU.add, [rr, self.lnpr], [rr])
            return
        dstf, dstr, fs, fsr, frow = fuse
        cks = [Reg() for _ in range(8)]
        for k in range(8):
            self.tt("dve", r[:, k, 0:tw], r[:, k, 0:tw], mean[:, 0:tw], ALU.subtract, [rr, meanr], [cks[k]])
            self.tt("dve", r[:, k, 0:tw], r[:, k, 0:tw], rstd[:, 0:tw], ALU.mult, [cks[k], rstdr], [cks[k]])
        for k in range(8):
            if dstf is not None:
                self.act(dstf(k), r[:, k, 0:tw], AF.Identity, [cks[k], fsr], [dstr], scale=fs[:, frow, k:k + 1],
                         bias=fs[:, frow + 1, k:k + 1])
            self.act(r[:, k, 0:tw], r[:, k, 0:tw], AF.Identity, [cks[k], self.lnpr], [cks[k]] + ([rr] if k == 7 else []),
                     scale=self.lnp[:, k, gi * 2 + l:gi * 2 + l + 1], bias=self.lnp[:, k, bi * 2 + l:bi * 2 + l + 1])

    def gated_resid(self, l, gchunk0, pY, pYr, m, t0, tw, xt, xtr, r, rr, L):
        modT, modr = self.modT, self.modr
        if t0 < TP:
            self.act(r[:, m, 0:tw], pY[:, 0:tw], AF.Identity, [pYr, modr], [rr], scale=modT[:, l, gchunk0 + m, 0:1])
        else:
            gb = AP(modT, l * 48 * 17 + (gchunk0 + m) * 17 + 1, [[2 * 48 * 17, 128], [1, 16], [0, 8]])
            self.tt("dve", r[:, m, 0:128].rearrange("p (b i) -> p b i", b=16),
                    pY[:, 0:128].rearrange("p (b i) -> p b i", b=16), gb, ALU.mult, [pYr, modr], [rr])
        self.stt("dve", r[:, m, 0:tw], xt[:, m, 0:tw], ALPHA, r[:, m, 0:tw], ALU.mult, ALU.add, [xtr, rr], [rr])

    def mlp_phase(self, l):
        S, dr = self.S, self.dr
        hT, hTg, cmr = self.hT, self.hTg, self.cmr
        w_in_l = self.w_in.ap()[l].rearrange("(k p) n -> p k n", p=128)
        GA0, GB0 = 5648, 6672
        wsrc = lambda w: w.ap()[l].rearrange("(k p) n -> p k n", p=128)
        with contextlib.ExitStack() as ph:
            sb = lambda name, shape, dt=F32: self.sb(ph, name, shape, dt)
            wts = [sb("wt%d" % i, [128, 8, 512], BF16) for i in range(4)]
            self.wi = 0

            pidx = {p_[0]: i_ for i_, p_ in enumerate(self.mlp_pieces(l))}

            def wload(key):
                wt, wr_ = wts[self.wi % 4]
                self.wi += 1
                pi = pidx[key]
                self.dma(wt[:], self.wsc.ap()[l, pi], [self.wscr[l][pi]], [wr_], q="sp")
                return wt, wr_
            oat, oatr = sb("oat", [128, 8, 512], BF16)
            obt, obtr = sb("obt", [128, 8, 512], BF16)
            xt, xtr = sb("xt", [128, 8, 512])
            r, rr = sb("r", [128, 8, 512])
            mixed, mixedr = sb("mixed", [128, 8, 512], BF16)
            h2, h2r = sb("h2", [128, 8, 512], BF16)
            upT, upTr = sb("upT", [128, 32, 512], BF16)
            sg, sgr = sb("sg", [128, 512])
            mA, mAr = sb("mA", [128, 512])
            mB, mBr = sb("mB", [128, 512])
            ur, urr = sb("ur", [128, 512])
            L = {}
            L["sq2"] = [(sg, sgr), (ur, urr)]
            L["mean"], L["meanr"] = sb("mean", [128, 512])
            L["rstd"], L["rstdr"] = sb("rstd", [128, 512])
            yst1 = sb("yst", [128, D])
            yst = [(yst1[0][:], yst1[1])] * 2
            self.rot = [4, 5, 6, 7]
            fs, fsr = sb("fs", [128, 4, 8])
            lnp, lnpr, modT, modr = self.lnp, self.lnpr, self.modT, self.modr
            mcol = lambda ll, c0: AP(modT, ll * 48 * 17 + c0 * 17, [[2 * 48 * 17, 128], [17, 8]])
            lrow = lambda rw: AP(lnp, rw, [[64, 128], [8, 8]])
            self.tt("dve", fs[:, 0, :], lrow(0 * 2 + l), mcol(l, 32), ALU.mult, [lnpr, modr], [fsr])
            self.tt("dve", fs[:, 1, :], lrow(1 * 2 + l), mcol(l, 32), ALU.mult, [lnpr, modr], [fsr])
            self.tt("dve", fs[:, 1, :], fs[:, 1, :], mcol(l, 24), ALU.add, [fsr, modr], [fsr])
            if l == 0:
                self.tt("dve", fs[:, 2, :], lrow(2 * 2 + l), mcol(1, 8), ALU.mult, [lnpr, modr], [fsr])
                self.tt("dve", fs[:, 3, :], lrow(3 * 2 + l), mcol(1, 8), ALU.mult, [lnpr, modr], [fsr])
                self.tt("dve", fs[:, 3, :], fs[:, 3, :], mcol(1, 0), ALU.add, [fsr, modr], [fsr])
            if self.debug:
                self.free_probe("mlp")
            def st_merge(gi):
                t0, tw = GROUPS[gi]
                hr = hTg[gi]
                self.dma(oat[:, :, 0:tw], self.oa.ap()[:, :, t0:t0 + tw].rearrange("h p t -> p h t"), [dr["oa"]], [oatr])
                self.dma(obt[:, :, 0:tw], self.ob.ap()[:, :, t0:t0 + tw].rearrange("h p t -> p h t"), [dr["ob"]], [obtr])
                hold = [(mA, mAr), (mB, mBr), (L["mean"], L["meanr"]), (L["rstd"], L["rstdr"])]
                for mh in range(2):
                    for br in range(2):
                        if br == 0:
                            wp_t, wp_r = wload(("pa", mh))
                            wg_t, wg_r = wload(("ga", mh))
                            src, srcr = oat, oatr
                        else:
                            wp_t, wp_r = wload(("pb", mh))
                            wg_t, wg_r = wload(("gb", mh))
                            src, srcr = obt, obtr
                        for mmi in range(4):
                            m = mh * 4 + mmi
                            ms_ = slice(mmi * 128, (mmi + 1) * 128)
                            pP, pPr = self.bank()
                            for k in range(8):
                                self.mm(pP[:, 0:tw], wp_t[:, k, ms_], src[:, k, 0:tw], k == 0, k == 7, [wp_r, srcr], [pPr])
                            pG, pGr = self.bank()
                            for k in range(8):
                                self.mm(pG[:, 0:tw], wg_t[:, k, ms_], hT[:, k, t0:t0 + tw], k == 0, k == 7, [wg_r, hr], [pGr])
                            self.act(sg[:, 0:tw], pG[:, 0:tw], AF.Sigmoid, [pGr], [sgr])
                            hd, hdr = hold[mmi]
                            if br == 0:
                                self.tt("dve", hd[:, 0:tw], pP[:, 0:tw], sg[:, 0:tw], ALU.mult, [pPr, sgr], [hdr])
                            else:
                                self.tt("dve", ur[:, 0:tw], pP[:, 0:tw], sg[:, 0:tw], ALU.mult, [pPr, sgr], [urr])
                                self.tt("pool", mixed[:, m, 0:tw], hd[:, 0:tw], ur[:, 0:tw], ALU.add, [hdr, urr], [mixedr])

            def st_rest(gi):
                t0, tw = GROUPS[gi]
                hr = hTg[gi]
                self.dma(xt[:, :, 0:tw], self.xs.ap()[:, :, t0:t0 + tw], [dr["xs"]], [xtr])
                for mh in range(2):
                    wo_t, wo_r = wload(("wo", mh))
                    for mmi in range(4):
                        m = mh * 4 + mmi
                        pY, pYr = self.bank()
                        for k in range(8):
                            self.mm(pY[:, 0:tw], wo_t[:, k, mmi * 128:(mmi + 1) * 128], mixed[:, k, 0:tw], k == 0, k == 7,
                                    [wo_r, mixedr], [pYr])
                        self.gated_resid(l, 16, pY, pYr, m, t0, tw, xt, xtr, r, rr, L)
                if t0 < TP:
                    self.layer_norm(r, rr, tw, 0, 1, l, L, fuse=(lambda k: h2[:, k, 0:tw], h2r, fs, fsr, 0))
                else:
                    self.layer_norm(r, rr, tw, 0, 1, l, L)
                    self.modulate(l, 1, r, rr, t0, tw, _Off(h2, t0), h2r, self.modT, self.modr, ph)
                for fg in range(8):
                    wu_t, wu_r = wload(("wu", fg))
                    for ff in range(4):
                        pU, pUr = self.bank()
                        for k in range(8):
                            self.mm(pU[:, 0:tw], wu_t[:, k, ff * 128:(ff + 1) * 128], h2[:, k, 0:tw], k == 0, k == 7,
                                    [wu_r, h2r], [pUr])
                        self.act(ur[:, 0:tw], pU[:, 0:tw], AF.Relu, [pUr], [urr])
                        self.tt("pool", upT[:, fg * 4 + ff, 0:tw], ur[:, 0:tw], ur[:, 0:tw], ALU.mult, [urr], [upTr])
                wdv = self.w_down.ap()[l].rearrange("(fc p) n -> p fc n", p=128)
                for mh in range(2):
                    for fg in range(4):
                        wd_t, wd_r = wload(("wd", mh, fg))
                        for fc in range(8):
                            f = fg * 8 + fc
                            for mmi in range(4):
                                pD, pDr = self.bank(mmi)
                                self.mm(pD[:, 0:tw], wd_t[:, fc, mmi * 128:(mmi + 1) * 128], upT[:, f, 0:tw], f == 0, f == 31,
                                        [wd_r, upTr], [pDr])
                    for mmi in range(4):
                        pD, pDr = self.bank(mmi)
                        self.gated_resid(l, 40, pD, pDr, mh * 4 + mmi, t0, tw, r, rr, xt, xtr, L)

            def st_ln2(gi):
                t0, tw = GROUPS[gi]
                hr = hTg[gi]
                if t0 < TP and l == 0:
                    self.layer_norm(xt, xtr, tw, 2, 3, l, L, fuse=(lambda k: hT[:, k, t0:t0 + tw], hr, fs, fsr, 2))
                elif t0 < TP:
                    self.layer_norm(xt, xtr, tw, 2, 3, l, L, fuse=(None, None, fs, fsr, 2))
                else:
                    self.layer_norm(xt, xtr, tw, 2, 3, l, L)
                if l == 0:
                    self.dma(self.xs.ap()[:, :, t0:t0 + tw], xt[:, :, 0:tw], [xtr], [dr["xs"]])
                    if t0 >= TP:
                        self.modulate(1, 0, xt, xtr, t0, tw, hT, hr, self.modT, self.modr, None)
                else:
                    for sub in range(tw // 128):
                        ys, ysr = yst[sub % 2]
                        for half in range(2):
                            pb, pr = self.bank()
                            for kk in range(4):
                                k = half * 4 + kk
                                self.tr(pb[:, kk * 128:(kk + 1) * 128], xt[:, k, sub * 128:(sub + 1) * 128], 128, [xtr], [pr])
                            self.cp("act", ys[:, half * 512:(half + 1) * 512], pb[:], [pr], [ysr])
                        self.dma(self.y_tok.ap()[t0 + sub * 128:t0 + (sub + 1) * 128, :], ys, [ysr], [dr["y_tok"]])

            st_merge(0)
            for gi in range(len(GROUPS)):
                st_rest(gi)
                if gi + 1 < len(GROUPS):
                    st_merge(gi + 1)
                st_ln2(gi)
            self.rot = list(range(8))
            S.flush()


class _Off:
    def __init__(self, t, t0):
        self.t, self.t0 = t, t0

    def __getitem__(self, idx):
        p, k, sl = idx
        return self.t[p, k, sl.start - self.t0:sl.stop - self.t0]


def _consts():
    i = np.arange(128)
    blk = i // 8
    c = np.zeros((16, 128, 128), np.float32)
    c[0] = np.eye(128)
    c[1] = 1.0
    c[2] = (i[:, None] <= i[None, :])
    c[3] = (i[:, None] > i[None, :])
    same = blk[:, None] == blk[None, :]
    c[4] = c[2] * same
    c[5] = c[3] * same
    c[6] = np.where(i[:, None] > i[None, :], 0.0, -1e4)
    c[7] = np.where((i[:, None] > i[None, :]) & same, 0.0, -1e4)
    c[8] = 1.0 / 1024
    c[9] = (i[:, None] // 16 == i[None, :] // 16)
    for n_, b_ in enumerate((16, 32, 64)):
        m_ = ((i[:, None] // (2 * b_) == i[None, :] // (2 * b_)) & (i[:, None] % (2 * b_) >= b_) & (i[None, :] % (2 * b_) < b_))
        c[10 + n_] = m_
        c[13 + n_] = m_.T
    small = np.zeros((128, 64), np.float32)
    small[i, blk] = 1.0
    for b in range(16):
        for j in range(3):
            small[8 * b + 5 + j, 16 + 3 * b + j] = 1.0
    sel = np.zeros((8, 8, 128), np.float32)
    for h in range(8):
        sel[h, h, :] = 1.0
    def bucket(n):
        n = max(n, 0)
        if n < 16:
            return n
        v = 16 + int(np.float32(np.log(np.float32(max(n, 16)) / np.float32(16)) / np.float32(math.log(128 / 16)) * np.float32(16)))
        return min(v, 31)
    oh = np.zeros((32, 384), np.float32)
    neg = np.full((16, 384), NEG, np.float32)
    for mpos in range(128, 256):
        oh[bucket(255 - mpos), mpos] = 1.0
        neg[:, mpos] = 0.0
    return c, small, sel.reshape(8, 1024), oh, neg


_NC = None


def kernel(x_prompt, x_sample, state_delta, state_conv, cache_k, cache_v, c_prompt, c_sample,
           rel_bias, w_ada, b_ada, w_in, w_conv, a_log, dt_bias, w_onorm, sinks,
           w_pa, w_pb, w_out, ln1_g, ln1_b, w_up, w_down, ln2_g, ln2_b):
    global _NC
    f = lambda a: np.ascontiguousarray(np.asarray(a, dtype=np.float32))
    if _NC is None:
        _NC = K().build()
    nc = _NC
    cm, small, sel, oh, neg = _consts()
    shared = {
        "rel_bias": f(rel_bias), "w_ada": f(w_ada), "b_ada": f(b_ada), "w_in": f(w_in),
        "w_conv": f(w_conv).reshape(8, 3072), "a_log": f(a_log).reshape(1, 16), "dt_bias": f(dt_bias).reshape(1, 16),
        "w_onorm": f(w_onorm).reshape(1, 256), "sinks": f(sinks).reshape(1, 32),
        "w_pa": f(w_pa), "w_pb": f(w_pb), "w_out": f(w_out),
        "ln_all": f(np.stack([f(ln1_g), f(ln1_b), f(ln2_g), f(ln2_b)], 0)).reshape(8, 1024),
        "w_up": f(w_up), "w_down": f(w_down),
        "cmat": cm, "csmall": small, "csel": sel, "coh": oh, "cneg": neg,
    }
    x_prompt, x_sample = f(x_prompt), f(x_sample)
    state_delta, state_conv, cache_k, cache_v = f(state_delta), f(state_conv), f(cache_k), f(cache_v)
    c_prompt, c_sample = f(c_prompt), f(c_sample)
    in_maps = []
    for c in range(NCORES):
        bs = slice(16 * c, 16 * c + 16)
        m = dict(shared)
        m["x_tok"] = np.ascontiguousarray(np.concatenate([x_prompt[c], x_sample[bs].reshape(128, D)], 0))
        m["c_all"] = np.ascontiguousarray(np.concatenate([c_prompt[c:c + 1], c_sample[bs]], 0))
        m["sd_in"] = np.ascontiguousarray(state_delta[:, bs])
        m["sc_in"] = np.ascontiguousarray(state_conv[:, bs].reshape(2, 48, 3072))
        m["ck_in"] = np.ascontiguousarray(cache_k[:, bs].reshape(2, 16, 128, 256))
        m["cv_in"] = np.ascontiguousarray(cache_v[:, bs].reshape(2, 16, 128, 256))
        in_maps.append(m)
    res = run_bass_kernel_spmd(nc, in_maps, core_ids=list(range(NCORES)))
    R = res.results
    yp = np.stack([R[c]["y_tok"][:TP] for c in range(NCORES)], 0)
    ys = np.concatenate([R[c]["y_tok"][TP:].reshape(16, 8, D) for c in range(NCORES)], 0)
    p_sd = np.stack([R[c]["p_sd"] for c in range(NCORES)], 1)
    p_sc = np.stack([R[c]["p_sc"] for c in range(NCORES)], 1)
    p_ck = np.stack([R[c]["p_ck"].reshape(2, 128, 4, 64) for c in range(NCORES)], 1)
    p_cv = np.stack([R[c]["p_cv"].reshape(2, 128, 4, 64) for c in range(NCORES)], 1)
    s_sd = np.concatenate([R[c]["s_sd"] for c in range(NCORES)], 1)
    s_sc = np.concatenate([R[c]["s_sc"].reshape(2, 16, 3, 3072) for c in range(NCORES)], 1)
    s_ck = np.concatenate([R[c]["s_ck"].reshape(2, 16, 128, 4, 64) for c in range(NCORES)], 1)
    s_cv = np.concatenate([R[c]["s_cv"].reshape(2, 16, 128, 4, 64) for c in range(NCORES)], 1)
    return tuple(np.ascontiguousarray(a.astype(np.float32)) for a in (yp, ys, p_sd, p_sc, p_ck, p_cv, s_sd, s_sc, s_ck, s_cv))
```
